# Optimizing a Trainium2 kernel written in Bass

```python
import math
import jax
import jax.numpy as jnp
from jax import lax
import numpy as np

D_MODEL = 4096
BATCH = 1
SEQ = 8192
DEPTH = 1
DEC_BATCH = 2
DEC_SEQ = 8192
PAST_LEN = 128

E_A = 2048
G_A = 8
CHUNK = 128
HEAD_DIM = 128
HEADS_PER_GROUP = 16
WINDOWS = ((128, 1), (512, 4), (2048, 16))
N_GROUPS_B = 3
N_HEADS_B = N_GROUPS_B * HEADS_PER_GROUP
QKV_W = N_HEADS_B * HEAD_DIM
E_B = HEADS_PER_GROUP * HEAD_DIM
N_BUCKETS = 32
REL_MAX_DISTANCE = 1024
DEEPNORM_ALPHA = (2.0 * DEPTH) ** 0.25
DEEPNORM_BETA = (8.0 * DEPTH) ** -0.25
LN_EPS = 1e-5
NEG_INF = -1e30

OFF_V = E_A
OFF_ZA = 2 * E_A
OFF_Q = 3 * E_A
OFF_K = OFF_Q + QKV_W
OFF_VB = OFF_K + QKV_W
OFF_ZB = OFF_VB + QKV_W
OFF_GA = OFF_ZB + E_B
OFF_GB = OFF_GA + D_MODEL
N_IN_COLS = OFF_GB + D_MODEL

kernel_name = "gated_parallel_gmlp_dilated_attn_encoder"


def layer_norm(x, g, b):
    xf = x.astype(jnp.float32)
    mu = jnp.mean(xf, axis=-1, keepdims=True)
    xc = xf - mu
    var = jnp.mean(xc * xc, axis=-1, keepdims=True)
    y = xc * lax.rsqrt(var + LN_EPS) * g.astype(jnp.float32) + b.astype(jnp.float32)
    return y.astype(x.dtype)


def t5_bucket(rel):
    half = N_BUCKETS // 2
    max_exact = half // 2
    ret = (rel > 0).astype(np.int32) * half
    n = np.abs(rel)
    nf = np.maximum(n, 1).astype(np.float64)
    large = max_exact + (np.log(nf / max_exact) / math.log(REL_MAX_DISTANCE / max_exact)
                         * (half - max_exact)).astype(np.int32)
    large = np.minimum(large, half - 1)
    return (ret + np.where(n < max_exact, n, large)).astype(np.int32)


def dilated_window_attention(q, k, v, bias_table, dilation, n_side):
    B, S, H, Dh = q.shape
    L = S // dilation
    blk = n_side
    nb = -(-L // blk)
    Lp = nb * blk
    f32 = jnp.float32
    qs = jnp.pad(q.reshape(B, L, dilation, H, Dh).astype(f32),
                 ((0, 0), (0, Lp - L), (0, 0), (0, 0), (0, 0)))
    qb = qs.reshape(B, nb, blk, dilation, H, Dh) * (Dh ** -0.5)

    def key_blocks(t):
        tp = jnp.pad(t.reshape(B, L, dilation, H, Dh).astype(f32),
                     ((0, 0), (blk, Lp - L + blk), (0, 0), (0, 0), (0, 0)))
        tp = tp.reshape(B, nb + 2, blk, dilation, H, Dh)
        return jnp.concatenate([tp[:, :-2], tp[:, 1:-1], tp[:, 2:]], axis=2)

    kb = key_blocks(k)
    vb = key_blocks(v)
    t_idx = np.arange(blk)[:, None]
    s_idx = np.arange(3 * blk)[None, :]
    off = s_idx - blk - t_idx
    band = np.abs(off) <= n_side
    key_m = np.arange(nb)[:, None] * blk - blk + np.arange(3 * blk)[None, :]
    in_range = (key_m >= 0) & (key_m < L)
    valid = band[None, :, :] & in_range[:, None, :]
    bucket = t5_bucket(dilation * off)
    bias = jnp.transpose(bias_table[bucket], (2, 0, 1)).astype(f32)

    logits = jnp.einsum('bnqrhd,bnkrhd->bnrhqk', qb, kb) + bias
    logits = jnp.where(valid[None, :, None, None], logits, NEG_INF)
    mx = jnp.max(logits, axis=-1, keepdims=True)
    p = jnp.exp(logits - mx)
    den = jnp.sum(p, axis=-1)
    o = jnp.einsum('bnrhqk,bnkrhd->bnqrhd', p, vb)
    den_t = jnp.transpose(den, (0, 1, 4, 2, 3))
    o = o / den_t[..., None]
    lse = jnp.transpose(mx[..., 0] + jnp.log(den), (0, 1, 4, 2, 3))
    o = o.reshape(B, Lp, dilation, H, Dh)[:, :L].reshape(B, S, H, Dh)
    lse = lse.reshape(B, Lp, dilation, H)[:, :L].reshape(B, S, H)
    return o, lse


def encoder_layer(x, w_in, w_spatial, b_spatial, ln_v_gain, ln_v_bias, rel_bias,
                  w_proj_a, w_proj_b, w_out, ln_gain, ln_bias):
    B, S, _ = x.shape
    h = jnp.matmul(x, w_in)
    u, v, za, q, k, vv, zb, ga, gb = jnp.split(
        h, [OFF_V, OFF_ZA, OFF_Q, OFF_K, OFF_VB, OFF_ZB, OFF_GA, OFF_GB], axis=-1)

    u = jax.nn.gelu(u, approximate=False)
    v = layer_norm(jax.nn.gelu(v, approximate=False), ln_v_gain, ln_v_bias)
    vc = v.reshape(B, S // CHUNK, CHUNK, G_A, E_A // G_A)
    sv = jnp.einsum('gpq,bnqgc->bnpgc', w_spatial, vc) + jnp.transpose(b_spatial)[:, :, None]
    a = u * sv.reshape(B, S, E_A) * jax.nn.silu(za)
    pa = jnp.matmul(a, w_proj_a)

    q = q.reshape(B, S, N_GROUPS_B, HEADS_PER_GROUP, HEAD_DIM)
    k = k.reshape(B, S, N_GROUPS_B, HEADS_PER_GROUP, HEAD_DIM)
    vv = vv.reshape(B, S, N_GROUPS_B, HEADS_PER_GROUP, HEAD_DIM)
    outs = []
    lses = []
    for g, (window, dilation) in enumerate(WINDOWS):
        o_g, lse_g = dilated_window_attention(
            q[:, :, g], k[:, :, g], vv[:, :, g],
            rel_bias[:, g * HEADS_PER_GROUP:(g + 1) * HEADS_PER_GROUP],
            dilation, window // (2 * dilation))
        outs.append(o_g)
        lses.append(lse_g)
    o_all = jnp.stack(outs, axis=0)
    w_grp = jax.nn.softmax(jnp.stack(lses, axis=0), axis=0)
    ob = jnp.sum(w_grp[..., None] * o_all, axis=0).reshape(B, S, E_B).astype(x.dtype)
    ob = ob * jax.nn.silu(zb)
    pb = jnp.matmul(ob, w_proj_b)

    merged = jax.nn.sigmoid(ga) * pa + jax.nn.sigmoid(gb) * pb
    out = jnp.matmul(merged, w_out)
    return layer_norm(DEEPNORM_ALPHA * x + out, ln_gain, ln_bias)


def encoder(x, w_in, w_spatial, b_spatial, ln_v_gain, ln_v_bias, rel_bias,
            w_proj_a, w_proj_b, w_out, ln_gain, ln_bias):
    for l in range(DEPTH):
        x = encoder_layer(x, w_in[l], w_spatial[l], b_spatial[l], ln_v_gain[l], ln_v_bias[l],
                          rel_bias, w_proj_a[l], w_proj_b[l], w_out[l], ln_gain[l], ln_bias[l])
    return x


def setup_inputs(seed: int = 0) -> dict:
    key = jax.random.key(seed)
    ks = jax.random.split(key, 14)
    f32 = jnp.float32
    x_prompt = jax.random.normal(ks[0], (BATCH, SEQ, D_MODEL), f32)
    x_sample = jax.random.normal(ks[1], (DEC_BATCH, DEC_SEQ, D_MODEL), f32)
    w_in = jax.random.normal(ks[2], (DEPTH, D_MODEL, N_IN_COLS), f32) * D_MODEL ** -0.5
    w_spatial = jax.random.normal(ks[3], (DEPTH, G_A, CHUNK, CHUNK), f32) * CHUNK ** -0.5
    b_spatial = 1.0 + 0.02 * jax.random.normal(ks[4], (DEPTH, G_A, CHUNK), f32)
    ln_v_gain = 1.0 + 0.02 * jax.random.normal(ks[5], (DEPTH, E_A), f32)
    ln_v_bias = 0.02 * jax.random.normal(ks[6], (DEPTH, E_A), f32)
    rel_bias = 0.5 * jax.random.normal(ks[7], (N_BUCKETS, N_HEADS_B), f32)
    w_proj_a = jax.random.normal(ks[8], (DEPTH, E_A, D_MODEL), f32) * (E_A ** -0.5 * DEEPNORM_BETA)
    w_proj_b = jax.random.normal(ks[9], (DEPTH, E_B, D_MODEL), f32) * (E_B ** -0.5 * DEEPNORM_BETA)
    w_out = jax.random.normal(ks[10], (DEPTH, D_MODEL, D_MODEL), f32) * (D_MODEL ** -0.5 * DEEPNORM_BETA)
    ln_gain = 1.0 + 0.02 * jax.random.normal(ks[11], (DEPTH, D_MODEL), f32)
    ln_bias = 0.02 * jax.random.normal(ks[12], (DEPTH, D_MODEL), f32)
    return {"x_prompt": x_prompt, "x_sample": x_sample, "w_in": w_in, "w_spatial": w_spatial,
            "b_spatial": b_spatial, "ln_v_gain": ln_v_gain, "ln_v_bias": ln_v_bias,
            "rel_bias": rel_bias, "w_proj_a": w_proj_a, "w_proj_b": w_proj_b, "w_out": w_out,
            "ln_gain": ln_gain, "ln_bias": ln_bias}


def reference(x_prompt, x_sample, w_in, w_spatial, b_spatial, ln_v_gain, ln_v_bias, rel_bias,
              w_proj_a, w_proj_b, w_out, ln_gain, ln_bias):
    y_prompt = encoder(x_prompt, w_in, w_spatial, b_spatial, ln_v_gain, ln_v_bias, rel_bias,
                       w_proj_a, w_proj_b, w_out, ln_gain, ln_bias)
    y_sample = encoder(x_sample, w_in, w_spatial, b_spatial, ln_v_gain, ln_v_bias, rel_bias,
                       w_proj_a, w_proj_b, w_out, ln_gain, ln_bias)
    return (y_prompt, y_sample)
```

```python
import math
from contextlib import ExitStack
import numpy as np
import concourse.bass as bass
import concourse.mybir as mybir
from concourse.bass_utils import run_bass_kernel_spmd

F32 = mybir.dt.float32
BF16 = mybir.dt.bfloat16
AF = mybir.ActivationFunctionType
ALU = mybir.AluOpType

CFG = dict(D=4096, EA=2048, HPG=16)
NCORE = 8
SEQ = 8192
NSEQ = 3
TOK = SEQ * NSEQ
OWN = TOK // NCORE
HALO = 1024
EXT = OWN + 2 * HALO
T = 512
NT_EXT = EXT // T
DIL = (1, 4, 16)
BIG = 32768.0
LN_EPS = 1e-5
N_BUCKETS = 32
REL_MAX_DISTANCE = 1024


def t5_bucket(rel):
    half = N_BUCKETS // 2
    max_exact = half // 2
    ret = (rel > 0).astype(np.int32) * half
    n = np.abs(rel)
    nf = np.maximum(n, 1).astype(np.float64)
    large = max_exact + (np.log(nf / max_exact) / math.log(REL_MAX_DISTANCE / max_exact)
                         * (half - max_exact)).astype(np.int32)
    large = np.minimum(large, half - 1)
    return (ret + np.where(n < max_exact, n, large)).astype(np.int32)


def make_onehot():
    oh = np.zeros((3, 32, 128, 2, 128), np.float32)
    k = np.arange(128)[:, None]
    q = np.arange(128)[None, :]
    for g, d in enumerate(DIL):
        for ab in range(2):
            delta = k - 64 - q if ab == 0 else k + 64 - q
            valid = np.abs(delta) <= 64
            bk = t5_bucket(d * delta)
            for b in range(32):
                oh[g, b, :, ab, :] = (valid & (bk == b)).astype(np.float32)
    return oh.reshape(3, 32, 128 * 256)


class Prog:
    def __init__(self, nc):
        self.nc = nc
        self.q = {k: [] for k in ("pe", "act", "dve", "sp", "pool")}

    def emit(self, eng, fn):
        self.q[eng].append(fn)

    def wait(self, eng, sem, val):
        if val <= 0:
            return
        self.q[eng].append(lambda e, sem=sem, val=val: e.wait_ge(sem, val))


def build(cfg):
    D, EA, HPG = cfg["D"], cfg["EA"], cfg["HPG"]
    KC = D // 128
    KA = EA // 128
    GA = EA // 256
    EBW = HPG * 128
    NH = 3 * HPG
    QKV = NH * 128
    OFF_V, OFF_ZA, OFF_Q = EA, 2 * EA, 3 * EA
    OFF_K = OFF_Q + QKV
    OFF_VB = OFF_K + QKV
    OFF_ZB = OFF_VB + QKV
    OFF_GA = OFF_ZB + EBW
    OFF_GB = OFF_GA + D
    NCOL = OFF_GB + D
    NG = NCOL // 512
    KMAX = max(KC, KA)
    alpha = (2.0 * 1) ** 0.25
    scale = 128 ** -0.5

    nc = bass.Bass("TRN2", target_bir_lowering=False)

    def din(name, shape):
        return nc.dram_tensor(name, list(shape), F32, kind="ExternalInput").ap()

    xT_d = din("xT", [128, KC, EXT])
    xown_d = din("xown", [OWN, D])
    sk_d = din("sk", [2, EXT])
    sq_d = din("sq", [2, OWN])
    W1_d = din("W1", [NG, 128, KC, 512])
    Wpa_d = din("Wpa", [D // 512, 128, KA, 512])
    Wpb_d = din("Wpb", [D // 512, 128, HPG, 512])
    Wout_d = din("Wout", [D // 512, 128, KC, 512])
    WsT_d = din("WsT", [128, GA * 128])
    bsp_d = din("bsp", [1, GA * 128])
    lnvg_d = din("lnvg", [1, EA])
    lnvb_d = din("lnvb", [1, EA])
    lng_d = din("lng", [1, D])
    lnb_d = din("lnb", [1, D])
    relb_d = din("relb", [32, NH])
    OH_d = din("OH", [3, 32, 32768])
    y_d = nc.dram_tensor("y", [OWN, D], F32, kind="ExternalOutput").ap()

    KT_d = nc.dram_tensor("KTs", [NH, 128, EXT], BF16, kind="Internal").ap()
    QT_d = nc.dram_tensor("QTs", [NH, 128, OWN], BF16, kind="Internal").ap()
    Vs_d = nc.dram_tensor("Vss", [3, EXT, EBW], BF16, kind="Internal").ap()
    AT_d = nc.dram_tensor("ATs", [KA, 128, OWN], BF16, kind="Internal").ap()
    OT_d = nc.dram_tensor("OTs", [HPG, 128, OWN], BF16, kind="Internal").ap()
    EB_d = nc.dram_tensor("EBs", [3, HPG, 32768], F32, kind="Internal").ap()

    P = Prog(nc)

    with ExitStack() as es1:
        PS = es1.enter_context(nc.psum_tensor("PS", [128, 8, 512], F32))
        wring = es1.enter_context(nc.sbuf_tensor("wring", [128, 2, KMAX * 512], BF16))
        MM = es1.enter_context(nc.semaphore("MM"))
        EV = es1.enter_context(nc.semaphore("EV"))
        WL0 = es1.enter_context(nc.semaphore("WL0"))
        WL1 = es1.enter_context(nc.semaphore("WL1"))
        XL = es1.enter_context(nc.semaphore("XL"))
        SL = es1.enter_context(nc.semaphore("SL"))
        SD0 = es1.enter_context(nc.semaphore("SD0"))
        SD1 = es1.enter_context(nc.semaphore("SD1"))
        SD2 = es1.enter_context(nc.semaphore("SD2"))
        SD3 = es1.enter_context(nc.semaphore("SD3"))
        CH = es1.enter_context(nc.semaphore("CH"))
        PL = es1.enter_context(nc.semaphore("PL"))
        WL = [WL0, WL1]
        st = dict(jn=0, ev=0, wn=[0, 0], slot_job=[-1, -1], bank_ev=[0, 0], wcount=0,
                  xl=0, sl=0, sd=[0, 0, 0, 0], ch=0, pl=0, pending_x=None, since_x=0)
        SD = [SD0, SD1, SD2, SD3]

        def step(eng, fn, waits=()):
            P.wait(eng, EV, st["ev"])
            for (sem, val) in waits:
                P.wait(eng, sem, val)
            P.emit(eng, lambda e, fn=fn: fn(e).then_inc(EV, 1))
            st["ev"] += 1
            return st["ev"]

        def xload(dst, src, after_jobs):
            st["xl"] += 1
            val = st["xl"] * 16

            def doit(extra=()):
                P.wait("pool", MM, after_jobs)
                for (sem, v) in extra:
                    P.wait("pool", sem, v)
                P.emit("pool", lambda e: e.dma_start(out=dst, in_=src).then_inc(XL, 16))
            return val, doit

        def job(wsrc, kcn, mm_fn, evac_fn, pe_waits=(), nbanks=4):
            jn = st["jn"]
            bs = jn % 2
            if wsrc is not None:
                slot = st["wcount"] % 2
                st["wcount"] += 1
                P.wait("pool", MM, st["slot_job"][slot] + 1)
                wdst = wring[:, slot, 0:kcn * 512].rearrange("p (k c) -> p k c", c=512)
                P.emit("pool", lambda e, wdst=wdst, wsrc=wsrc, slot=slot:
                       e.dma_start(out=wdst, in_=wsrc).then_inc(WL[slot], 16))
                st["wn"][slot] += 1
                st["slot_job"][slot] = jn
                P.wait("pe", WL[slot], 16 * st["wn"][slot])
                wt = wdst
                if st["pending_x"] is not None:
                    st["since_x"] += 1
                    if st["since_x"] >= 2:
                        st["pending_x"]()
                        st["pending_x"] = None
            else:
                wt = None
            P.wait("pe", EV, st["bank_ev"][bs])
            for (sem, val) in pe_waits:
                P.wait("pe", sem, val)
            banks = [PS[:, bs * 4 + i, :] for i in range(4)]
            mms = mm_fn(wt, banks)
            for m in mms[:-1]:
                P.emit("pe", m)
            P.emit("pe", lambda e, m=mms[-1]: m(e).then_inc(MM, 1))
            st["jn"] += 1
            psv = PS[:, bs * 4:(bs + 1) * 4, :]
            evac_fn(psv, [(MM, jn + 1)])
            st["bank_ev"][bs] = st["ev"]
            if wsrc is not None and st.get("p0_on"):
                p0_tick()
            return jn

        def flush_x():
            if st["pending_x"] is not None:
                st["pending_x"]()
                st["pending_x"] = None

        def sl(start, n, stp):
            return slice(start, start + stp * (n - 1) + 1, stp)

        def mm(out, lhsT, rhs, start, stop, skip=False):
            if skip:
                return lambda e: e.matmul(out, lhsT, rhs, start=start, stop=stop, skip_group_check=True)
            return lambda e: e.matmul(out, lhsT, rhs, start=start, stop=stop)

        def fm_mms(wt, banks, xt, kcn, ntok=512):
            out = []
            for s in range(4):
                for kc in range(kcn):
                    out.append(mm(banks[s][:, 0:ntok], wt[:, kc, 128 * s:128 * s + 128], xt[:, kc, 0:ntok],
                                  kc == 0, kc == kcn - 1))
            return out

        def tm_mms(wt, banks, lhs_fn, kcn, tbs=(0, 1, 2, 3)):
            out = []
            for tb in tbs:
                for kc in range(kcn):
                    out.append(mm(banks[tb], lhs_fn(kc, tb), wt[:, kc, :], kc == 0, kc == kcn - 1))
            return out

        def sp_dma(dst, src, sdi, waits):
            for (sem, val) in waits:
                P.wait("sp", sem, val)
            P.emit("sp", lambda e: e.dma_start(out=dst, in_=src).then_inc(SD[sdi], 16))
            st["sd"][sdi] += 1
            return st["sd"][sdi] * 16

        def sp_load(dst, src, waits=()):
            for (sem, val) in waits:
                P.wait("sp", sem, val)
            P.emit("sp", lambda e: e.dma_start(out=dst, in_=src).then_inc(SL, 16))
            st["sl"] += 1
            return st["sl"] * 16

        def pool_load(dst, src, waits=()):
            for (sem, val) in waits:
                P.wait("pool", sem, val)
            P.emit("pool", lambda e: e.dma_start(out=dst, in_=src).then_inc(PL, 16))
            st["pl"] += 1
            return st["pl"] * 16

        relb_sb = es1.enter_context(nc.sbuf_tensor("relb_s", [32, NH], F32))
        etab = es1.enter_context(nc.sbuf_tensor("etab", [32, NH], F32))
        ohs = [es1.enter_context(nc.sbuf_tensor("ohsb%d" % i, [32, 512], F32)) for i in range(2)]
        ebsts = [es1.enter_context(nc.sbuf_tensor("ebst%d" % i, [HPG, 512], F32)) for i in range(2)]
        OHL = [es1.enter_context(nc.semaphore("OHL%d" % i)) for i in range(2)]
        SDE = [es1.enter_context(nc.semaphore("SDE%d" % i)) for i in range(2)]
        p0state = dict(done=[0, 0], active=False)

        def p0_gen():
            P.emit("sp", lambda e: e.dma_start(out=relb_sb[:], in_=relb_d[:, :]).then_inc(SDE[0], 16))
            s_exp = step("act", lambda e: e.activation(out=etab[:], in_=relb_sb[:], func=AF.Exp),
                         waits=[(SDE[0], 16)])
            chunks = [(g, chn) for g in range(3) for chn in range(64)]
            ohl = [0, 0]
            sdv = [16, 0]
            jobs0 = []

            def oh_load(ci):
                g, chn = chunks[ci]
                b = ci % 2
                if ci >= 2:
                    P.wait("sp", MM, jobs0[ci - 2] + 1)
                P.emit("sp", lambda e, b=b, g=g, chn=chn: e.dma_start(out=ohs[b][:], in_=OH_d[g, :, chn * 512:(chn + 1) * 512])
                       .then_inc(OHL[b], 16))
                ohl[b] += 16
                return ohl[b]
            lvals = {0: oh_load(0)}
            for ci, (g, chn) in enumerate(chunks):
                b = ci % 2

                def mmf(wt, banks, g=g, b=b):
                    return [mm(banks[0][0:HPG, :], etab[:, g * HPG:(g + 1) * HPG], ohs[b][:], True, True)]

                def evf(psv, w, g=g, chn=chn, b=b):
                    s1 = step("dve", lambda e: e.tensor_copy(out=ebsts[b][:], in_=psv[0:HPG, 0, :]),
                              waits=w + [(SDE[b], sdv[b])])
                    P.wait("sp", EV, s1)
                    P.emit("sp", lambda e: e.dma_start(out=EB_d[g, :, chn * 512:(chn + 1) * 512], in_=ebsts[b][:])
                           .then_inc(SDE[b], 16))
                    sdv[b] += 16
                    p0state["done"][b] = sdv[b]
                jobs0.append(st["jn"])
                p0state["active"] = True
                job(None, 0, mmf, evf, pe_waits=[(OHL[b], lvals[ci]), (EV, s_exp)])
                p0state["active"] = False
                if ci + 1 < len(chunks):
                    lvals[ci + 1] = oh_load(ci + 1)
                yield
        p0 = p0_gen()

        def p0_tick():
            if not p0state["active"]:
                next(p0, None)
        marks = [{k: len(v) for k, v in P.q.items()}]

        with ExitStack() as es3:
            xt1 = es3.enter_context(nc.sbuf_tensor("xt1", [128, KC, 512], BF16))
            G = es3.enter_context(nc.sbuf_tensor("G", [128, 4 * EA], F32))
            vn = es3.enter_context(nc.sbuf_tensor("vn", [128, 4, EA], BF16))
            aT = es3.enter_context(nc.sbuf_tensor("aT", [128, KA, 512], BF16))
            TU = es3.enter_context(nc.sbuf_tensor("TU", [128, 4, 512], F32))
            stg = es3.enter_context(nc.sbuf_tensor("stg", [128, 2, 4, 512], BF16))
            WsT = es3.enter_context(nc.sbuf_tensor("WsT_s", [128, GA * 128], BF16))
            bsp = es3.enter_context(nc.sbuf_tensor("bsp_s", [128, GA * 128], F32))
            lnvg = es3.enter_context(nc.sbuf_tensor("lnvg_s", [128, EA], F32))
            lnvb = es3.enter_context(nc.sbuf_tensor("lnvb_s", [128, EA], F32))
            stats = es3.enter_context(nc.sbuf_tensor("stats", [128, 8, 6], F32))
            mv = es3.enter_context(nc.sbuf_tensor("mv", [128, 2], F32))
            rstd = es3.enter_context(nc.sbuf_tensor("rstd", [128, 1], F32))
            st["p0_on"] = True
            gv = G[:].rearrange("p (t c) -> p t c", t=4)
            SA = G[:].rearrange("p (c t) -> p c t", t=512)
            p0w = []
            c1 = pool_load(WsT[:], WsT_d[:, :], waits=p0w)
            c2 = sp_load(bsp[:], bsp_d.partition_broadcast(128)[:, 0, :], waits=p0w)
            c2 = sp_load(lnvg[:], lnvg_d.partition_broadcast(128)[:, 0, :])
            c2 = sp_load(lnvb[:], lnvb_d.partition_broadcast(128)[:, 0, :])
            stg_sd = [0, 0]
            stg_i = [0]
            at_sd = [0]

            def out_stage(write_fn_eng, write_fn, dsts_fn, w):
                i = stg_i[0] % 2
                stg_i[0] += 1
                sdi = 1 if i == 0 else 3
                s1 = step(write_fn_eng, lambda e, i=i: write_fn(e, stg[:, i]), waits=w + [(SD[sdi], stg_sd[i])])
                v = 0
                for (dst, src_fn) in dsts_fn():
                    v = sp_dma(dst, src_fn(stg[:, i]), sdi, [(EV, s1)])
                stg_sd[i] = v

            for e_ in range(NT_EXT):
                own = 2 <= e_ <= 7
                t0 = (e_ - 2) * T
                xv, xdo = xload(xt1[:], xT_d[:, :, e_ * T:(e_ + 1) * T], st["jn"])
                if e_ == 0:
                    xdo()
                else:
                    st["pending_x"] = xdo
                    st["since_x"] = 0
                xw = [(XL, xv)]
                if own:
                    for cg in range(EA // 512):
                        def evf(psv, w, cg=cg):
                            step("act", lambda e: e.activation(out=gv[:, :, cg * 512:(cg + 1) * 512], in_=psv,
                                                               func=AF.Gelu), waits=w)
                        job(W1_d[OFF_V // 512 + cg], KC,
                            lambda wt, banks: tm_mms(wt, banks, lambda kc, tb: xt1[:, kc, 128 * tb:128 * tb + 128], KC),
                            evf, pe_waits=xw)
                    for tb in range(4):
                        nch = EA // 512
                        for chn in range(nch):
                            step("dve", lambda e, tb=tb, chn=chn: e.bn_stats(out=stats[:, chn, :],
                                                                           in_=gv[:, tb, chn * 512:(chn + 1) * 512]))
                        step("dve", lambda e: e.bn_aggr(out=mv[:], in_=stats[:, 0:nch, :]))
                        step("dve", lambda e: e.tensor_scalar_add(out=rstd[:], in0=mv[:, 1:2], scalar1=LN_EPS))
                        step("act", lambda e: e.sqrt(out=rstd[:], in_=rstd[:]))
                        step("dve", lambda e: e.reciprocal(out=rstd[:], in_=rstd[:]))
                        step("dve", lambda e, tb=tb: e.scalar_tensor_tensor(out=gv[:, tb, :], in0=gv[:, tb, :],
                                                                           scalar=mv[:, 0:1], in1=lnvg[:],
                                                                           op0=ALU.subtract, op1=ALU.mult),
                             waits=[(SL, c2)])
                        s_vn = step("dve", lambda e, tb=tb: e.scalar_tensor_tensor(out=vn[:, tb, :], in0=gv[:, tb, :],
                                                                                  scalar=rstd[:, 0:1], in1=lnvb[:],
                                                                                  op0=ALU.mult, op1=ALU.add))
                    for qg in range(NH // 4):
                        def evf(psv, w, qg=qg, t0=t0):
                            out_stage("act", lambda e, s: e.mul(out=s, in_=psv, mul=scale),
                                      lambda: [(QT_d[4 * qg:4 * qg + 4, :, t0:t0 + T].rearrange("s p t -> p s t"),
                                                lambda s: s)], w)
                        job(W1_d[OFF_Q // 512 + qg], KC, lambda wt, banks: fm_mms(wt, banks, xt1, KC), evf, pe_waits=xw)
                    for sj in range(KA // 4):
                        def mmf(wt, banks, sj=sj):
                            out = []
                            for s in range(4):
                                cc = 4 * sj + s
                                g = cc // 2
                                for tb in range(4):
                                    out.append(mm(banks[s][:, 128 * tb:128 * tb + 128], vn[:, tb, cc * 128:(cc + 1) * 128],
                                                  WsT[:, g * 128:(g + 1) * 128], True, True, skip=True))
                            return out

                        def evf(psv, w, sj=sj):
                            for half in range(2):
                                g = 2 * sj + half
                                o = SA[:, 4 * sj + 2 * half:4 * sj + 2 * half + 2, :].rearrange("p s (t q) -> p s t q", q=128)
                                i0 = psv[:, 2 * half:2 * half + 2, :].rearrange("p s (t q) -> p s t q", q=128)
                                i1 = bsp[:, g * 128:(g + 1) * 128].rearrange("p (a b q) -> p a b q", a=1, b=1) \
                                    .to_broadcast([128, 2, 4, 128])
                                step("dve", lambda e, o=o, i0=i0, i1=i1: e.tensor_tensor(out=o, in0=i0, in1=i1, op=ALU.add),
                                     waits=w if half == 0 else [])
                        job(None, 0, mmf, evf, pe_waits=[(EV, s_vn), (PL, c1)])
                    for cg in range(EA // 512):
                        def evf(psv, w, cg=cg):
                            step("act", lambda e: e.activation(out=TU[:], in_=psv, func=AF.Gelu), waits=w)
                            step("dve", lambda e: e.tensor_tensor(out=SA[:, 4 * cg:4 * cg + 4, :],
                                                                  in0=SA[:, 4 * cg:4 * cg + 4, :], in1=TU[:], op=ALU.mult))
                        job(W1_d[cg], KC, lambda wt, banks: fm_mms(wt, banks, xt1, KC), evf, pe_waits=xw)
                    for cg in range(EA // 512):
                        def evf(psv, w, cg=cg):
                            step("act", lambda e: e.activation(out=TU[:], in_=psv, func=AF.Silu), waits=w)
                            return step("dve", lambda e: e.tensor_tensor(out=aT[:, 4 * cg:4 * cg + 4, :],
                                                                         in0=SA[:, 4 * cg:4 * cg + 4, :], in1=TU[:],
                                                                         op=ALU.mult),
                                        waits=[(SD[2], at_sd[0])] if cg == 0 else [])
                        hold = {}

                        def evf3(psv, w, evf=evf, hold=hold):
                            hold["s"] = evf(psv, w)
                        job(W1_d[OFF_ZA // 512 + cg], KC, lambda wt, banks: fm_mms(wt, banks, xt1, KC), evf3, pe_waits=xw)
                    at_sd[0] = sp_dma(AT_d[:, :, t0:t0 + T].rearrange("k p t -> p k t"), aT[:], 2, [(EV, hold["s"])])
                groups = [0, 1, 2] if 1 <= e_ <= 8 else [2]
                for g in groups:
                    for hq in range(HPG // 4):
                        kg = g * (HPG // 4) + hq

                        lo = 0
                        nt_ = T
                        if g == 0 and e_ == 1:
                            lo, nt_ = T - 128, 128
                        if g == 0 and e_ == 8:
                            lo, nt_ = 0, 128

                        def evf(psv, w, kg=kg, e_=e_, lo=lo, nt_=nt_):
                            out_stage("dve", lambda e, s: e.tensor_copy(out=s[:, :, 0:nt_], in_=psv[:, :, 0:nt_]),
                                      lambda: [(KT_d[4 * kg:4 * kg + 4, :, e_ * T + lo:e_ * T + lo + nt_]
                                                .rearrange("s p t -> p s t"), lambda s: s[:, :, 0:nt_])], w)
                        job(W1_d[OFF_K // 512 + kg], KC,
                            lambda wt, banks, lo=lo, nt_=nt_: fm_mms(wt, banks, xt1[:, :, lo:lo + nt_], KC, ntok=nt_),
                            evf, pe_waits=xw)
                for g in groups:
                    for hq in range(HPG // 4):
                        vg = g * (HPG // 4) + hq
                        if g == 0:
                            lf = lambda kc, tb: xt1[:, kc, 128 * tb:128 * tb + 128]
                        else:
                            lf = lambda kc, tb: xt1[:, kc, tb::4]

                        tbs = [0, 1, 2, 3]
                        if g == 0 and e_ == 1:
                            tbs = [3]
                        if g == 0 and e_ == 8:
                            tbs = [0]

                        def dsts(g=g, hq=hq, e_=e_, tbs=tbs):
                            cs = slice(hq * 512, (hq + 1) * 512)
                            if g == 0:
                                t_lo, t_n = tbs[0], len(tbs)
                                return [(Vs_d[0, e_ * T + 128 * t_lo:e_ * T + 128 * (t_lo + t_n), cs]
                                         .rearrange("(t p) c -> p t c", p=128), lambda s: s[:, t_lo:t_lo + t_n, :])]
                            if g == 1:
                                return [(Vs_d[1].rearrange("(r n) c -> n r c", r=4)[128 * e_:128 * e_ + 128, :, cs],
                                         lambda s: s)]
                            v2 = Vs_d[2].rearrange("(r n) c -> n r c", r=16)
                            return [(v2[32 * e_:32 * e_ + 32, 4 * i:4 * i + 4, cs], (lambda s, i=i: s[i::4]))
                                    for i in range(4)]

                        def evf(psv, w, dsts=dsts, tbs=tbs):
                            a, b_ = tbs[0], tbs[0] + len(tbs)
                            out_stage("act", lambda e, s: e.copy(out=s[:, a:b_, :], in_=psv[:, a:b_, :]), dsts, w)
                        job(W1_d[OFF_VB // 512 + vg], KC,
                            lambda wt, banks, lf=lf, tbs=tbs: tm_mms(wt, banks, lf, KC, tbs=tbs), evf, pe_waits=xw)
                flush_x()
            st["p0_on"] = False
            for _ in p0:
                pass
            p1_sd1 = st["sd"][1] * 16
            p1_sd2 = st["sd"][2] * 16
            p1_sd3 = st["sd"][3] * 16
            p1_ev = st["ev"]
            p1_jn = st["jn"]
        marks.append({k: len(v) for k, v in P.q.items()})

        n1 = EXT // 4
        NS = NE = NP = NO = 3
        with ExitStack() as es4:
            Q = [es4.enter_context(nc.semaphore("Q%d" % i)) for i in range(6)]
            QF = es4.enter_context(nc.semaphore("QF"))
            LD = [es4.enter_context(nc.semaphore("LD%d" % i)) for i in range(2)]
            kts = [es4.enter_context(nc.sbuf_tensor("kt%d" % i, [128, EXT], BF16)) for i in range(2)]
            qts = [es4.enter_context(nc.sbuf_tensor("qt%d" % i, [128, OWN], BF16)) for i in range(2)]
            vbs = [es4.enter_context(nc.sbuf_tensor("vb%d" % i, [128, 50 * 128], BF16)) for i in range(2)]
            ebs = [es4.enter_context(nc.sbuf_tensor("eb%d" % i, [128, 1, 256], F32)) for i in range(2)]
            Efs = [es4.enter_context(nc.sbuf_tensor("Ef%d" % i, [128, 2, 256], F32)) for i in range(NE)]
            Pbs = [es4.enter_context(nc.sbuf_tensor("Pb%d" % i, [128, 2, 256], BF16)) for i in range(NP)]
            numaccs = [es4.enter_context(nc.sbuf_tensor("numacc%d" % i, [128, OWN], F32)) for i in range(2)]
            denaccs = [es4.enter_context(nc.sbuf_tensor("denacc%d" % i, [128, OWN], F32)) for i in range(2)]
            OSD = [es4.enter_context(nc.semaphore("OSD%d" % i)) for i in range(2)]
            sks = es4.enter_context(nc.sbuf_tensor("sks", [2, EXT], BF16))
            sqs = es4.enter_context(nc.sbuf_tensor("sqs", [2, OWN], BF16))
            ones = es4.enter_context(nc.sbuf_tensor("ones", [128, 128], BF16))

            p1w = [(EV, p1_ev), (MM, p1_jn), (SD[1], p1_sd1), (SD[2], p1_sd2), (SD[3], p1_sd3),
                   (SDE[0], p0state["done"][0]), (SDE[1], p0state["done"][1])]
            m1 = pool_load(sks[:], sk_d[:, :], waits=p1w)
            m2 = pool_load(sqs[:], sq_d[:, :])
            for (sem, val) in p1w:
                P.wait("sp", sem, val)
            for (sem, val) in p1w:
                P.wait("dve", sem, val)
            P.emit("dve", lambda e: e.memset(ones[:], 1.0).then_inc(QF, 1))
            qf = 1
            for eng in ("pe", "act"):
                for (sem, val) in p1w:
                    P.wait(eng, sem, val)

            units = []
            ldcount = [0, 0]
            hg_last_unit = {}
            hgi = 0
            for h in range(HPG):
                for g in range(3):
                    set_ = hgi % 2
                    kt, qt, vb, eb = kts[set_], qts[set_], vbs[set_], ebs[set_]
                    H = g * HPG + h
                    hc = slice(h * 128, (h + 1) * 128)
                    loads = [(kt[:], KT_d[H]), (qt[:], QT_d[H]),
                             (eb[:, 0, :], EB_d[g, h].rearrange("(k c) -> k c", c=256))]
                    ul = []
                    if g == 0:
                        vt0 = vb[:, 0:25 * 128].rearrange("p (i c) -> p i c", c=128)
                        loads.append((vt0, Vs_d[0, 960:960 + 25 * 128, hc].rearrange("(i p) c -> p i c", p=128)))
                        for u in range(12):
                            blocks = []
                            for i in range(2):
                                j = 2 * u + i
                                blocks.append((slice(128 * j, 128 * j + 128), 128,
                                               [(slice(960 + 128 * j, 1088 + 128 * j), 128, vt0[:, j, :]),
                                                (slice(1088 + 128 * j, 1216 + 128 * j), 128, vt0[:, j + 1, :])]))
                            ul.append((blocks, (lambda t, u=u: t[:, 256 * u:256 * u + 256]), 256))
                    elif g == 1:
                        vt1 = vb[:, 0:28 * 128].rearrange("p (r i c) -> p r i c", r=4, c=128)
                        for r in range(4):
                            loads.append((vt1[:, r], Vs_d[1, r * n1 + 192:r * n1 + 192 + 7 * 128, hc]
                                          .rearrange("(i p) c -> p i c", p=128)))
                        for r in range(4):
                            for pi in range(3):
                                blocks = []
                                for i in range(2):
                                    b = 2 * pi + i
                                    ka = r + 4 * (192 + 128 * b)
                                    kb = r + 4 * (320 + 128 * b)
                                    blocks.append((sl(r + 512 * b, 128, 4), 128,
                                                   [(sl(ka, 128, 4), 128, vt1[:, r, b, :]),
                                                    (sl(kb, 128, 4), 128, vt1[:, r, b + 1, :])]))
                                ul.append((blocks, (lambda t, r=r, pi=pi: t[:, sl(r + 1024 * pi, 256, 4)]), 256))
                    else:
                        vt2 = vb[:, 0:32 * 128].rearrange("p (r j c) -> p r j c", r=16, c=128)
                        vt2c = vb[0:64, 32 * 128:48 * 128].rearrange("p (r c) -> p r c", c=128)
                        v2 = Vs_d[2].rearrange("(r n) c -> n r c", r=16)
                        for j in range(2):
                            loads.append((vt2[:, :, j, :], v2[128 * j:128 * j + 128, :, hc]))
                        loads.append((vt2c, v2[256:320, :, hc]))
                        for r in range(16):
                            blocks = []
                            for qb, nq in ((0, 128), (1, 64)):
                                kl = []
                                for ab in range(2):
                                    j = qb + ab
                                    nk = 128 if j < 2 else 64
                                    vt = vt2[:, r, j, :] if j < 2 else vt2c[:, r, :]
                                    kl.append((sl(r + 2048 * j, nk, 16), nk, vt))
                                blocks.append((sl(r + 2048 * qb, nq, 16), nq, kl))
                            ul.append((blocks, (lambda t, r=r: t[:, sl(r, 192, 16)]), 192))
                    for ui, (blocks, accout, ncols) in enumerate(ul):
                        units.append(dict(hgi=hgi, set=set_, g=g, h=h, blocks=blocks, accout=accout, ncols=ncols,
                                          first=(ui == 0), loads=loads if ui == 0 else None,
                                          last_head=(g == 2 and ui == len(ul) - 1)))
                    hg_last_unit[hgi] = len(units) - 1
                    hgi += 1
            NU = len(units)
            ldval = {}

            def s1(n):
                u = units[n]
                kt, qt = kts[u["set"]], qts[u["set"]]
                if u["first"]:
                    i = u["hgi"]
                    if i >= 2:
                        P.wait("sp", Q[4], hg_last_unit[i - 2] + 1)
                    for (dst, src) in u["loads"]:
                        P.emit("sp", lambda e, dst=dst, src=src, s=u["set"]: e.dma_start(out=dst, in_=src).then_inc(LD[s], 16))
                        ldcount[u["set"]] += 1
                    P.wait("pe", LD[u["set"]], 16 * ldcount[u["set"]])
                    if n == 0:
                        P.wait("pe", PL, m2)
                P.wait("pe", Q[2], n - NS + 1)
                Sb = PS[:, n % NS, :]
                fns = []
                for bi, (qs, nq, kl) in enumerate(u["blocks"]):
                    for ab, (ks, nk, vt) in enumerate(kl):
                        c0 = bi * 256 + ab * 128
                        fns.append(mm(Sb[0:nk, c0:c0 + nq], kt[:, ks], qt[:, qs], True, False, skip=True))
                        fns.append(mm(Sb[0:nk, c0:c0 + nq], sks[:, ks], sqs[:, qs], False, True, skip=True))
                for f in fns[:-1]:
                    P.emit("pe", f)
                P.emit("pe", lambda e, f=fns[-1]: f(e).then_inc(Q[1], 1))

            def s2(n):
                P.wait("act", Q[1], n + 1)
                P.wait("act", Q[3], n - NE + 1)
                Sb = PS[:, n % NS, :]
                Ef = Efs[n % NE]
                P.emit("act", lambda e: e.activation(out=Ef[:].rearrange("p b c -> p (b c)"), in_=Sb,
                                                     func=AF.Exp).then_inc(Q[2], 1))

            def s3(n):
                u = units[n]
                P.wait("dve", Q[2], n + 1)
                P.wait("dve", Q[4], n - NP + 1)
                Ef, Pb, eb = Efs[n % NE], Pbs[n % NP], ebs[u["set"]]
                P.emit("dve", lambda e: e.tensor_tensor(out=Pb[:], in0=Ef[:], in1=eb[:].to_broadcast([128, 2, 256]),
                                                        op=ALU.mult).then_inc(Q[3], 1))

            def s4(n):
                u = units[n]
                P.wait("pe", Q[3], n + 1)
                P.wait("pe", Q[5], n - NO + 1)
                OD = PS[:, NS + n % NO, :]
                Pb = Pbs[n % NP]
                fns = []
                k0 = True
                for bi, (qs, nq, kl) in enumerate(u["blocks"]):
                    for ab, (ks, nk, vt) in enumerate(kl):
                        rhs = Pb[0:nk, bi, ab * 128:ab * 128 + nq]
                        fns.append(mm(OD[:, bi * 128:bi * 128 + nq], vt[0:nk, :], rhs, k0, False, skip=True))
                        k0 = False
                        fns.append(mm(OD[:, 256 + bi * 128:256 + bi * 128 + nq], ones[0:nk, :], rhs, False, False, skip=True))
                for f in fns[:-1]:
                    P.emit("pe", f)
                P.emit("pe", lambda e, f=fns[-1]: f(e).then_inc(Q[4], 1))

            osd = [0, 0]

            def s5(n):
                nonlocal qf
                u = units[n]
                P.wait("dve", Q[4], n + 1)
                P.wait("dve", Q[5], n)
                OD = PS[:, NS + n % NO, :]
                nco = u["ncols"]
                ao = u["accout"]
                hb = u["h"] % 2
                numacc, denacc = numaccs[hb], denaccs[hb]
                if u["g"] == 0:
                    if u["first"]:
                        P.wait("dve", OSD[hb], osd[hb])
                    P.emit("dve", lambda e: e.tensor_copy(out=ao(numacc[:]), in_=OD[:, 0:nco]))
                    P.emit("dve", lambda e: e.tensor_copy(out=ao(denacc[:]), in_=OD[:, 256:256 + nco]).then_inc(Q[5], 1))
                else:
                    P.emit("dve", lambda e: e.tensor_tensor(out=ao(numacc[:]), in0=ao(numacc[:]), in1=OD[:, 0:nco],
                                                            op=ALU.add))
                    P.emit("dve", lambda e: e.tensor_tensor(out=ao(denacc[:]), in0=ao(denacc[:]),
                                                            in1=OD[:, 256:256 + nco], op=ALU.add).then_inc(Q[5], 1))
                if u["last_head"]:
                    h = u["h"]
                    P.wait("dve", Q[5], n + 1)
                    P.emit("dve", lambda e: e.reciprocal(out=denacc[:], in_=denacc[:]).then_inc(QF, 1))
                    qf += 1
                    P.wait("dve", QF, qf)
                    P.emit("dve", lambda e: e.tensor_tensor(out=numacc[:], in0=numacc[:], in1=denacc[:],
                                                            op=ALU.mult).then_inc(QF, 1))
                    qf += 1
                    P.wait("pool", QF, qf)
                    P.emit("pool", lambda e: e.dma_start(out=OT_d[h], in_=numacc[:]).then_inc(OSD[hb], 16))
                    osd[hb] += 16

            for t in range(NU + 4):
                if t < NU:
                    s1(t)
                if 0 <= t - 1 < NU:
                    s2(t - 1)
                if 0 <= t - 2 < NU:
                    s3(t - 2)
                if 0 <= t - 3 < NU:
                    s4(t - 3)
                if 0 <= t - 4 < NU:
                    s5(t - 4)
            P.wait("dve", QF, qf)
            P.emit("dve", lambda e: e.memset(ones[:], 1.0).then_inc(CH, 1))
            st["ch"] += 1
            P.wait("sp", Q[4], NU)
            p2_ch = st["ch"]
            p2_osd = list(osd)
        marks.append({k: len(v) for k, v in P.q.items()})

        AR = max(KC * 512 + KA * 512 + HPG * 512, 8 * D)
        RS = max(2 * D, 6144)
        with ExitStack() as es5:
            arena = es5.enter_context(nc.sbuf_tensor("arena", [128, AR], BF16))
            mT = es5.enter_context(nc.sbuf_tensor("mT", [128, KC, 512], BF16))
            R = es5.enter_context(nc.sbuf_tensor("R", [128, RS], F32))
            stats3 = es5.enter_context(nc.sbuf_tensor("stats3", [128, 8, 6], F32))
            mv3 = es5.enter_context(nc.sbuf_tensor("mv3", [128, 2], F32))
            rstd3 = es5.enter_context(nc.sbuf_tensor("rstd3", [128, 1], F32))
            xt3 = arena[:, 0:KC * 512].rearrange("p (k t) -> p k t", t=512)
            aT3 = arena[:, KC * 512:KC * 512 + KA * 512].rearrange("p (k t) -> p k t", t=512)
            obT = arena[:, KC * 512 + KA * 512:KC * 512 + KA * 512 + HPG * 512].rearrange("p (k t) -> p k t", t=512)
            z = arena[:, 0:8 * D].bitcast(F32).rearrange("p (t c) -> p t c", t=4)
            SG = R[:, 0:2048].rearrange("p (s t) -> p s t", t=512)
            TA = R[:, 2048:4096].rearrange("p (s t) -> p s t", t=512)
            SGB = R[:, 4096:6144].rearrange("p (s t) -> p s t", t=512)
            gainb = R[:, 0:D]
            biasb = R[:, D:2 * D]
            for eng in ("sp", "pool"):
                P.wait(eng, CH, p2_ch)
                for i in range(2):
                    P.wait(eng, OSD[i], p2_osd[i])
            P.wait("pe", CH, p2_ch)
            P.wait("act", CH, p2_ch)
            P.wait("dve", CH, p2_ch)
            y_sd = 0
            y_sd1 = 0
            y_ev = st["ev"]
            for ti in range(OWN // T):
                t0 = ti * T
                xv, xdo = xload(xt3, xT_d[:, :, HALO + t0:HALO + t0 + T], st["jn"])
                xdo([(SD[2], y_sd), (EV, y_ev)])
                la = sp_load(aT3, AT_d[:, :, t0:t0 + T].rearrange("k p t -> p k t"),
                             waits=[(SD[2], y_sd), (SD[1], y_sd1), (EV, st["ev"]), (MM, st["jn"])])
                la = sp_load(obT, OT_d[:, :, t0:t0 + T].rearrange("k p t -> p k t"))
                xw = [(XL, xv)]
                for cg in range(EBW // 512):
                    def evf(psv, w, cg=cg):
                        step("act", lambda e: e.activation(out=SG, in_=psv, func=AF.Silu), waits=w)
                        return step("dve", lambda e: e.tensor_tensor(out=obT[:, 4 * cg:4 * cg + 4, :],
                                                                     in0=obT[:, 4 * cg:4 * cg + 4, :], in1=SG, op=ALU.mult),
                                    waits=[(SL, la)])
                    hold = {}

                    def evf3(psv, w, evf=evf, hold=hold):
                        hold["s"] = evf(psv, w)
                    job(W1_d[OFF_ZB // 512 + cg], KC, lambda wt, banks: fm_mms(wt, banks, xt3, KC), evf3, pe_waits=xw)
                s_ob = hold["s"]
                for dg in range(D // 512):
                    def ev_ga(psv, w):
                        step("act", lambda e: e.activation(out=SG, in_=psv, func=AF.Sigmoid), waits=w)
                    job(W1_d[OFF_GA // 512 + dg], KC, lambda wt, banks: fm_mms(wt, banks, xt3, KC), ev_ga, pe_waits=xw)

                    def ev_gb(psv, w):
                        step("act", lambda e: e.activation(out=SGB, in_=psv, func=AF.Sigmoid), waits=w)
                    job(W1_d[OFF_GB // 512 + dg], KC, lambda wt, banks: fm_mms(wt, banks, xt3, KC), ev_gb, pe_waits=xw)

                    def ev_pa(psv, w):
                        step("dve", lambda e: e.tensor_tensor(out=TA, in0=SG, in1=psv, op=ALU.mult), waits=w)
                    job(Wpa_d[dg], KA, lambda wt, banks: fm_mms(wt, banks, aT3, KA), ev_pa, pe_waits=[(SL, la)])

                    def ev_pb(psv, w, dg=dg):
                        step("dve", lambda e: e.tensor_tensor(out=SGB, in0=SGB, in1=psv, op=ALU.mult), waits=w)
                        step("dve", lambda e: e.tensor_tensor(out=mT[:, 4 * dg:4 * dg + 4, :], in0=TA, in1=SGB, op=ALU.add))
                    job(Wpb_d[dg], HPG, lambda wt, banks: fm_mms(wt, banks, obT, HPG), ev_pb, pe_waits=[(EV, s_ob)])
                s_m = st["ev"]
                jn_c = st["jn"]
                lz = sp_load(z, xown_d[t0:t0 + T, :].rearrange("(t p) c -> p t c", p=128),
                             waits=[(EV, s_m), (MM, jn_c)])
                lz = sp_load(gainb, lng_d.partition_broadcast(128)[:, 0, :])
                lz = sp_load(biasb, lnb_d.partition_broadcast(128)[:, 0, :])
                for og in range(D // 512):
                    def evf(psv, w, og=og):
                        step("dve", lambda e: e.scalar_tensor_tensor(out=z[:, :, og * 512:(og + 1) * 512],
                                                                     in0=z[:, :, og * 512:(og + 1) * 512], scalar=alpha,
                                                                     in1=psv, op0=ALU.mult, op1=ALU.add),
                             waits=w + [(SL, lz)])
                    job(Wout_d[og], KC,
                        lambda wt, banks: tm_mms(wt, banks, lambda kc, tb: mT[:, kc, 128 * tb:128 * tb + 128], KC),
                        evf, pe_waits=[(EV, s_m)])
                for tb in range(4):
                    nch = D // 512
                    for chn in range(nch):
                        step("dve", lambda e, tb=tb, chn=chn: e.bn_stats(out=stats3[:, chn, :],
                                                                       in_=z[:, tb, chn * 512:(chn + 1) * 512]))
                    step("dve", lambda e: e.bn_aggr(out=mv3[:], in_=stats3[:, 0:nch, :]))
                    step("dve", lambda e: e.tensor_scalar_add(out=rstd3[:], in0=mv3[:, 1:2], scalar1=LN_EPS))
                    step("act", lambda e: e.sqrt(out=rstd3[:], in_=rstd3[:]))
                    step("dve", lambda e: e.reciprocal(out=rstd3[:], in_=rstd3[:]))
                    step("dve", lambda e, tb=tb: e.scalar_tensor_tensor(out=z[:, tb, :], in0=z[:, tb, :],
                                                                       scalar=mv3[:, 0:1], in1=gainb,
                                                                       op0=ALU.subtract, op1=ALU.mult))
                    s_y = step("dve", lambda e, tb=tb: e.scalar_tensor_tensor(out=z[:, tb, :], in0=z[:, tb, :],
                                                                             scalar=rstd3[:, 0:1], in1=biasb,
                                                                             op0=ALU.mult, op1=ALU.add))
                    if tb < 2:
                        y_sd = sp_dma(y_d[t0 + 128 * tb:t0 + 128 * tb + 128, :], z[:, tb, :], 2, [(EV, s_y)])
                        y_ev = s_y
                    else:
                        y_sd1 = sp_dma(y_d[t0 + 128 * tb:t0 + 128 * tb + 128, :], z[:, tb, :], 1, [(EV, s_y)])
            P.wait("sp", SD[2], y_sd)
            P.wait("sp", SD[1], y_sd1)

        marks.append({k: len(v) for k, v in P.q.items()})
        ph = cfg.get("phases", 4)
        lim = marks[ph - 1] if ph > 0 else {k: 0 for k in P.q}
        for k in P.q:
            P.q[k] = P.q[k][:lim[k]]
        with nc.Block() as block:
            @block.tensor
            def _(e):
                for f in P.q["pe"]:
                    f(e)

            @block.scalar
            def _(e):
                for f in P.q["act"]:
                    f(e)

            @block.vector
            def _(e):
                for f in P.q["dve"]:
                    f(e)

            @block.sync
            def _(e):
                for f in P.q["sp"]:
                    f(e)

            @block.gpsimd
            def _(e):
                for f in P.q["pool"]:
                    f(e)
    return nc


def prep_inputs(cfg, x_prompt, x_sample, w_in, w_spatial, b_spatial, ln_v_gain, ln_v_bias, rel_bias,
                w_proj_a, w_proj_b, w_out, ln_gain, ln_bias):
    D, EA, HPG = cfg["D"], cfg["EA"], cfg["HPG"]
    KC, KA, GA = D // 128, EA // 128, EA // 256
    f = np.float32
    xf = np.concatenate([np.asarray(x_prompt, f).reshape(-1, D), np.asarray(x_sample, f).reshape(-1, D)], axis=0)
    xpad = np.concatenate([np.zeros((HALO, D), f), xf, np.zeros((HALO, D), f)], axis=0)

    def gran(w, kcn):
        K, N = w.shape
        return np.ascontiguousarray(w.reshape(kcn, 128, N // 512, 512).transpose(2, 1, 0, 3))

    shared = {
        "W1": gran(np.asarray(w_in[0], f), KC),
        "Wpa": gran(np.asarray(w_proj_a[0], f), KA),
        "Wpb": gran(np.asarray(w_proj_b[0], f), HPG),
        "Wout": gran(np.asarray(w_out[0], f), KC),
        "WsT": np.ascontiguousarray(np.asarray(w_spatial[0], f).transpose(2, 0, 1).reshape(128, GA * 128)),
        "bsp": np.asarray(b_spatial[0], f).reshape(1, GA * 128),
        "lnvg": np.asarray(ln_v_gain[0], f).reshape(1, EA),
        "lnvb": np.asarray(ln_v_bias[0], f).reshape(1, EA),
        "lng": np.asarray(ln_gain[0], f).reshape(1, D),
        "lnb": np.asarray(ln_bias[0], f).reshape(1, D),
        "relb": np.ascontiguousarray(np.asarray(rel_bias, f)),
        "OH": make_onehot(),
    }
    in_maps = []
    for c in range(NCORE):
        ext = xpad[c * OWN:c * OWN + EXT]
        xT = np.ascontiguousarray(ext.T.reshape(KC, 128, EXT).transpose(1, 0, 2))
        flat = np.arange(EXT) + c * OWN - HALO
        seq = np.floor_divide(flat, SEQ)
        seq = np.where(flat < 0, -1, np.where(flat >= TOK, NSEQ, seq))
        s = (seq == seq[-1]).astype(f)
        sk = np.stack([s, 1.0 - s], axis=0)
        so = s[HALO:HALO + OWN]
        sq = np.stack([-BIG * (1.0 - so), -BIG * so], axis=0).astype(f)
        m = dict(shared)
        m.update({"xT": xT, "xown": np.ascontiguousarray(xf[c * OWN:(c + 1) * OWN]), "sk": sk.astype(f), "sq": sq})
        in_maps.append(m)
    return in_maps


def run(cfg, **inputs):
    nc = build(cfg)
    in_maps = prep_inputs(cfg, **inputs)
    res = run_bass_kernel_spmd(nc, in_maps, core_ids=list(range(NCORE)))
    y = np.concatenate([np.asarray(r["y"]) for r in res.results], axis=0)
    D = cfg["D"]
    nb = np.asarray(inputs["x_prompt"]).shape[0]
    yp = y[:nb * SEQ].reshape(nb, SEQ, D).astype(np.float32)
    ys = y[nb * SEQ:].reshape(-1, SEQ, D).astype(np.float32)
    return (yp, ys)


def kernel(**inputs):
    return run(CFG, **inputs)
```

```python
import math
from contextlib import ExitStack
import numpy as np
import concourse.bass as bass
import concourse.mybir as mybir
from concourse.bass_utils import run_bass_kernel_spmd

F32 = mybir.dt.float32
BF16 = mybir.dt.bfloat16
AF = mybir.ActivationFunctionType
ALU = mybir.AluOpType

CFG = dict(D=4096, EA=2048, HPG=16)
NCORE = 8
SEQ = 8192
NSEQ = 3
TOK = SEQ * NSEQ
OWN = TOK // NCORE
HALO = 1024
EXT = OWN + 2 * HALO
T = 512
NT_EXT = EXT // T
DIL = (1, 4, 16)
BIG = 32768.0
LN_EPS = 1e-5
N_BUCKETS = 32
REL_MAX_DISTANCE = 1024


def t5_bucket(rel):
    half = N_BUCKETS // 2
    max_exact = half // 2
    ret = (rel > 0).astype(np.int32) * half
    n = np.abs(rel)
    nf = np.maximum(n, 1).astype(np.float64)
    large = max_exact + (np.log(nf / max_exact) / math.log(REL_MAX_DISTANCE / max_exact)
                         * (half - max_exact)).astype(np.int32)
    large = np.minimum(large, half - 1)
    return (ret + np.where(n < max_exact, n, large)).astype(np.int32)


def make_onehot():
    oh = np.zeros((3, 32, 128, 2, 128), np.float32)
    k = np.arange(128)[:, None]
    q = np.arange(128)[None, :]
    for g, d in enumerate(DIL):
        for ab in range(2):
            delta = k - 64 - q if ab == 0 else k + 64 - q
            valid = np.abs(delta) <= 64
            bk = t5_bucket(d * delta)
            for b in range(32):
                oh[g, b, :, ab, :] = (valid & (bk == b)).astype(np.float32)
    return oh.reshape(3, 32, 128 * 256)


class Prog:
    def __init__(self, nc):
        self.nc = nc
        self.q = {k: [] for k in ("pe", "act", "dve", "sp", "pool")}

    def emit(self, eng, fn):
        self.q[eng].append(fn)

    def wait(self, eng, sem, val):
        if val <= 0:
            return
        self.q[eng].append(lambda e, sem=sem, val=val: e.wait_ge(sem, val))


def build(cfg):
    D, EA, HPG = cfg["D"], cfg["EA"], cfg["HPG"]
    KC = D // 128
    KA = EA // 128
    GA = EA // 256
    EBW = HPG * 128
    NH = 3 * HPG
    QKV = NH * 128
    OFF_V, OFF_ZA, OFF_Q = EA, 2 * EA, 3 * EA
    OFF_K = OFF_Q + QKV
    OFF_VB = OFF_K + QKV
    OFF_ZB = OFF_VB + QKV
    OFF_GA = OFF_ZB + EBW
    OFF_GB = OFF_GA + D
    NCOL = OFF_GB + D
    NG = NCOL // 512
    KMAX = max(KC, KA)
    alpha = (2.0 * 1) ** 0.25
    scale = 128 ** -0.5

    nc = bass.Bass("TRN2", target_bir_lowering=False)

    def din(name, shape):
        return nc.dram_tensor(name, list(shape), F32, kind="ExternalInput").ap()

    xT_d = din("xT", [128, KC, EXT])
    xown_d = din("xown", [OWN, D])
    sk_d = din("sk", [2, EXT])
    sq_d = din("sq", [2, OWN])
    W1_d = din("W1", [NG, 128, KC, 512])
    Wpa_d = din("Wpa", [D // 512, 128, KA, 512])
    Wpb_d = din("Wpb", [D // 512, 128, HPG, 512])
    Wout_d = din("Wout", [D // 512, 128, KC, 512])
    WsT_d = din("WsT", [128, GA * 128])
    bsp_d = din("bsp", [1, GA * 128])
    lnvg_d = din("lnvg", [1, EA])
    lnvb_d = din("lnvb", [1, EA])
    lng_d = din("lng", [1, D])
    lnb_d = din("lnb", [1, D])
    relb_d = din("relb", [32, NH])
    OH_d = din("OH", [3, 32, 32768])
    y_d = nc.dram_tensor("y", [OWN, D], F32, kind="ExternalOutput").ap()

    KT_d = nc.dram_tensor("KTs", [NH, 128, EXT], BF16, kind="Internal").ap()
    QT_d = nc.dram_tensor("QTs", [NH, 128, OWN], BF16, kind="Internal").ap()
    Vs_d = nc.dram_tensor("Vss", [3, EXT, EBW], BF16, kind="Internal").ap()
    AT_d = nc.dram_tensor("ATs", [KA, 128, OWN], BF16, kind="Internal").ap()
    OT_d = nc.dram_tensor("OTs", [HPG, 128, OWN], BF16, kind="Internal").ap()
    EB_d = nc.dram_tensor("EBs", [3, HPG, 32768], F32, kind="Internal").ap()

    P = Prog(nc)

    with ExitStack() as es1:
        PS = es1.enter_context(nc.psum_tensor("PS", [128, 8, 512], F32))
        wring = es1.enter_context(nc.sbuf_tensor("wring", [128, 2, KMAX * 512], BF16))
        MM = es1.enter_context(nc.semaphore("MM"))
        EV = es1.enter_context(nc.semaphore("EV"))
        WL0 = es1.enter_context(nc.semaphore("WL0"))
        WL1 = es1.enter_context(nc.semaphore("WL1"))
        XL = es1.enter_context(nc.semaphore("XL"))
        SL = es1.enter_context(nc.semaphore("SL"))
        SD0 = es1.enter_context(nc.semaphore("SD0"))
        SD1 = es1.enter_context(nc.semaphore("SD1"))
        SD2 = es1.enter_context(nc.semaphore("SD2"))
        SD3 = es1.enter_context(nc.semaphore("SD3"))
        CH = es1.enter_context(nc.semaphore("CH"))
        PL = es1.enter_context(nc.semaphore("PL"))
        WL = [WL0, WL1]
        st = dict(jn=0, ev=0, wn=[0, 0], slot_job=[-1, -1], bank_ev=[0, 0], wcount=0,
                  xl=0, sl=0, sd=[0, 0, 0, 0], ch=0, pl=0, pending_x=None, since_x=0)
        SD = [SD0, SD1, SD2, SD3]

        def step(eng, fn, waits=()):
            P.wait(eng, EV, st["ev"])
            for (sem, val) in waits:
                P.wait(eng, sem, val)
            P.emit(eng, lambda e, fn=fn: fn(e).then_inc(EV, 1))
            st["ev"] += 1
            return st["ev"]

        def xload(dst, src, after_jobs):
            st["xl"] += 1
            val = st["xl"] * 16

            def doit(extra=()):
                P.wait("pool", MM, after_jobs)
                for (sem, v) in extra:
                    P.wait("pool", sem, v)
                P.emit("pool", lambda e: e.dma_start(out=dst, in_=src).then_inc(XL, 16))
            return val, doit

        def job(wsrc, kcn, mm_fn, evac_fn, pe_waits=(), nbanks=4):
            jn = st["jn"]
            bs = jn % 2
            if wsrc is not None:
                slot = st["wcount"] % 2
                st["wcount"] += 1
                P.wait("pool", MM, st["slot_job"][slot] + 1)
                wdst = wring[:, slot, 0:kcn * 512].rearrange("p (k c) -> p k c", c=512)
                P.emit("pool", lambda e, wdst=wdst, wsrc=wsrc, slot=slot:
                       e.dma_start(out=wdst, in_=wsrc).then_inc(WL[slot], 16))
                st["wn"][slot] += 1
                st["slot_job"][slot] = jn
                P.wait("pe", WL[slot], 16 * st["wn"][slot])
                wt = wdst
                if st["pending_x"] is not None:
                    st["since_x"] += 1
                    if st["since_x"] >= 2:
                        st["pending_x"]()
                        st["pending_x"] = None
            else:
                wt = None
            P.wait("pe", EV, st["bank_ev"][bs])
            for (sem, val) in pe_waits:
                P.wait("pe", sem, val)
            banks = [PS[:, bs * 4 + i, :] for i in range(4)]
            mms = mm_fn(wt, banks)
            for m in mms[:-1]:
                P.emit("pe", m)
            P.emit("pe", lambda e, m=mms[-1]: m(e).then_inc(MM, 1))
            st["jn"] += 1
            psv = PS[:, bs * 4:(bs + 1) * 4, :]
            evac_fn(psv, [(MM, jn + 1)])
            st["bank_ev"][bs] = st["ev"]
            return jn

        def flush_x():
            if st["pending_x"] is not None:
                st["pending_x"]()
                st["pending_x"] = None

        def sl(start, n, stp):
            return slice(start, start + stp * (n - 1) + 1, stp)

        def mm(out, lhsT, rhs, start, stop, skip=False):
            if skip:
                return lambda e: e.matmul(out, lhsT, rhs, start=start, stop=stop, skip_group_check=True)
            return lambda e: e.matmul(out, lhsT, rhs, start=start, stop=stop)

        def fm_mms(wt, banks, xt, kcn, ntok=512):
            out = []
            for s in range(4):
                for kc in range(kcn):
                    out.append(mm(banks[s][:, 0:ntok], wt[:, kc, 128 * s:128 * s + 128], xt[:, kc, 0:ntok],
                                  kc == 0, kc == kcn - 1))
            return out

        def tm_mms(wt, banks, lhs_fn, kcn, tbs=(0, 1, 2, 3)):
            out = []
            for tb in tbs:
                for kc in range(kcn):
                    out.append(mm(banks[tb], lhs_fn(kc, tb), wt[:, kc, :], kc == 0, kc == kcn - 1))
            return out

        def sp_dma(dst, src, sdi, waits):
            for (sem, val) in waits:
                P.wait("sp", sem, val)
            P.emit("sp", lambda e: e.dma_start(out=dst, in_=src).then_inc(SD[sdi], 16))
            st["sd"][sdi] += 1
            return st["sd"][sdi] * 16

        def sp_load(dst, src, waits=()):
            for (sem, val) in waits:
                P.wait("sp", sem, val)
            P.emit("sp", lambda e: e.dma_start(out=dst, in_=src).then_inc(SL, 16))
            st["sl"] += 1
            return st["sl"] * 16

        def pool_load(dst, src, waits=()):
            for (sem, val) in waits:
                P.wait("pool", sem, val)
            P.emit("pool", lambda e: e.dma_start(out=dst, in_=src).then_inc(PL, 16))
            st["pl"] += 1
            return st["pl"] * 16

        with ExitStack() as es2:
            relb_sb = es2.enter_context(nc.sbuf_tensor("relb_s", [32, NH], F32))
            etab = es2.enter_context(nc.sbuf_tensor("etab", [32, NH], F32))
            ohs = [es2.enter_context(nc.sbuf_tensor("ohsb%d" % i, [32, 2048], F32)) for i in range(2)]
            ebsts = [es2.enter_context(nc.sbuf_tensor("ebst%d" % i, [HPG, 2048], F32)) for i in range(2)]
            OHL = [es2.enter_context(nc.semaphore("OHL%d" % i)) for i in range(2)]
            v0 = sp_load(relb_sb[:], relb_d[:, :])
            s_exp = step("act", lambda e: e.activation(out=etab[:], in_=relb_sb[:], func=AF.Exp),
                         waits=[(SL, v0)])
            chunks = [(g, chn) for g in range(3) for chn in range(16)]
            ohl = [0, 0]
            sdv = [0, 0]
            jobs0 = []

            def oh_load(ci):
                g, chn = chunks[ci]
                b = ci % 2
                if ci >= 2:
                    P.wait("sp", MM, jobs0[ci - 2] + 1)
                P.emit("sp", lambda e, b=b, g=g, chn=chn: e.dma_start(out=ohs[b][:], in_=OH_d[g, :, chn * 2048:(chn + 1) * 2048])
                       .then_inc(OHL[b], 16))
                ohl[b] += 16
                return ohl[b]
            lvals = {0: oh_load(0)}
            for ci, (g, chn) in enumerate(chunks):
                b = ci % 2

                def mmf(wt, banks, g=g, b=b):
                    return [mm(banks[i][0:HPG, :], etab[:, g * HPG:(g + 1) * HPG],
                               ohs[b][:, i * 512:(i + 1) * 512], True, True) for i in range(4)]

                def evf(psv, w, g=g, chn=chn, b=b):
                    sdi = 0 if b == 0 else 2
                    s1 = step("dve", lambda e: e.tensor_copy(out=ebsts[b][:].rearrange("p (b c) -> p b c", c=512),
                                                             in_=psv[0:HPG]),
                              waits=w + [(SD[sdi], sdv[b])])
                    sdv[b] = sp_dma(EB_d[g, :, chn * 2048:(chn + 1) * 2048], ebsts[b][:], sdi, [(EV, s1)])
                jobs0.append(st["jn"])
                if ci + 1 < len(chunks):
                    jobs0_len = len(jobs0)
                job(None, 0, mmf, evf, pe_waits=[(OHL[b], lvals[ci]), (EV, s_exp)])
                if ci + 1 < len(chunks):
                    lvals[ci + 1] = oh_load(ci + 1)
            eb_done0, eb_done2 = sdv[0], sdv[1]
        marks = [{k: len(v) for k, v in P.q.items()}]

        with ExitStack() as es3:
            xt1 = es3.enter_context(nc.sbuf_tensor("xt1", [128, KC, 512], BF16))
            G = es3.enter_context(nc.sbuf_tensor("G", [128, 4 * EA], F32))
            vn = es3.enter_context(nc.sbuf_tensor("vn", [128, 4, EA], BF16))
            aT = es3.enter_context(nc.sbuf_tensor("aT", [128, KA, 512], BF16))
            TU = es3.enter_context(nc.sbuf_tensor("TU", [128, 4, 512], F32))
            stg = es3.enter_context(nc.sbuf_tensor("stg", [128, 2, 4, 512], BF16))
            WsT = es3.enter_context(nc.sbuf_tensor("WsT_s", [128, GA * 128], BF16))
            bsp = es3.enter_context(nc.sbuf_tensor("bsp_s", [128, GA * 128], F32))
            lnvg = es3.enter_context(nc.sbuf_tensor("lnvg_s", [128, EA], F32))
            lnvb = es3.enter_context(nc.sbuf_tensor("lnvb_s", [128, EA], F32))
            stats = es3.enter_context(nc.sbuf_tensor("stats", [128, 8, 6], F32))
            mv = es3.enter_context(nc.sbuf_tensor("mv", [128, 2], F32))
            rstd = es3.enter_context(nc.sbuf_tensor("rstd", [128, 1], F32))
            gv = G[:].rearrange("p (t c) -> p t c", t=4)
            SA = G[:].rearrange("p (c t) -> p c t", t=512)
            p0w = [(MM, st["jn"]), (EV, st["ev"]), (SD[0], eb_done0), (SD[2], eb_done2)]
            c1 = pool_load(WsT[:], WsT_d[:, :], waits=p0w)
            c2 = sp_load(bsp[:], bsp_d.partition_broadcast(128)[:, 0, :], waits=p0w)
            c2 = sp_load(lnvg[:], lnvg_d.partition_broadcast(128)[:, 0, :])
            c2 = sp_load(lnvb[:], lnvb_d.partition_broadcast(128)[:, 0, :])
            stg_sd = [0, 0]
            stg_i = [0]
            at_sd = [0]

            def out_stage(write_fn_eng, write_fn, dsts_fn, w):
                i = stg_i[0] % 2
                stg_i[0] += 1
                sdi = 1 if i == 0 else 3
                s1 = step(write_fn_eng, lambda e, i=i: write_fn(e, stg[:, i]), waits=w + [(SD[sdi], stg_sd[i])])
                v = 0
                for (dst, src_fn) in dsts_fn():
                    v = sp_dma(dst, src_fn(stg[:, i]), sdi, [(EV, s1)])
                stg_sd[i] = v

            for e_ in range(NT_EXT):
                own = 2 <= e_ <= 7
                t0 = (e_ - 2) * T
                xv, xdo = xload(xt1[:], xT_d[:, :, e_ * T:(e_ + 1) * T], st["jn"])
                if e_ == 0:
                    xdo()
                else:
                    st["pending_x"] = xdo
                    st["since_x"] = 0
                xw = [(XL, xv)]
                if own:
                    for cg in range(EA // 512):
                        def evf(psv, w, cg=cg):
                            step("act", lambda e: e.activation(out=gv[:, :, cg * 512:(cg + 1) * 512], in_=psv,
                                                               func=AF.Gelu), waits=w)
                        job(W1_d[OFF_V // 512 + cg], KC,
                            lambda wt, banks: tm_mms(wt, banks, lambda kc, tb: xt1[:, kc, 128 * tb:128 * tb + 128], KC),
                            evf, pe_waits=xw)
                    for tb in range(4):
                        nch = EA // 512
                        for chn in range(nch):
                            step("dve", lambda e, tb=tb, chn=chn: e.bn_stats(out=stats[:, chn, :],
                                                                           in_=gv[:, tb, chn * 512:(chn + 1) * 512]))
                        step("dve", lambda e: e.bn_aggr(out=mv[:], in_=stats[:, 0:nch, :]))
                        step("dve", lambda e: e.tensor_scalar_add(out=rstd[:], in0=mv[:, 1:2], scalar1=LN_EPS))
                        step("act", lambda e: e.sqrt(out=rstd[:], in_=rstd[:]))
                        step("dve", lambda e: e.reciprocal(out=rstd[:], in_=rstd[:]))
                        step("dve", lambda e, tb=tb: e.scalar_tensor_tensor(out=gv[:, tb, :], in0=gv[:, tb, :],
                                                                           scalar=mv[:, 0:1], in1=lnvg[:],
                                                                           op0=ALU.subtract, op1=ALU.mult),
                             waits=[(SL, c2)])
                        s_vn = step("dve", lambda e, tb=tb: e.scalar_tensor_tensor(out=vn[:, tb, :], in0=gv[:, tb, :],
                                                                                  scalar=rstd[:, 0:1], in1=lnvb[:],
                                                                                  op0=ALU.mult, op1=ALU.add))
                    for qg in range(NH // 4):
                        def evf(psv, w, qg=qg, t0=t0):
                            out_stage("act", lambda e, s: e.mul(out=s, in_=psv, mul=scale),
                                      lambda: [(QT_d[4 * qg:4 * qg + 4, :, t0:t0 + T].rearrange("s p t -> p s t"),
                                                lambda s: s)], w)
                        job(W1_d[OFF_Q // 512 + qg], KC, lambda wt, banks: fm_mms(wt, banks, xt1, KC), evf, pe_waits=xw)
                    for sj in range(KA // 4):
                        def mmf(wt, banks, sj=sj):
                            out = []
                            for s in range(4):
                                cc = 4 * sj + s
                                g = cc // 2
                                for tb in range(4):
                                    out.append(mm(banks[s][:, 128 * tb:128 * tb + 128], vn[:, tb, cc * 128:(cc + 1) * 128],
                                                  WsT[:, g * 128:(g + 1) * 128], True, True, skip=True))
                            return out

                        def evf(psv, w, sj=sj):
                            for half in range(2):
                                g = 2 * sj + half
                                o = SA[:, 4 * sj + 2 * half:4 * sj + 2 * half + 2, :].rearrange("p s (t q) -> p s t q", q=128)
                                i0 = psv[:, 2 * half:2 * half + 2, :].rearrange("p s (t q) -> p s t q", q=128)
                                i1 = bsp[:, g * 128:(g + 1) * 128].rearrange("p (a b q) -> p a b q", a=1, b=1) \
                                    .to_broadcast([128, 2, 4, 128])
                                step("dve", lambda e, o=o, i0=i0, i1=i1: e.tensor_tensor(out=o, in0=i0, in1=i1, op=ALU.add),
                                     waits=w if half == 0 else [])
                        job(None, 0, mmf, evf, pe_waits=[(EV, s_vn), (PL, c1)])
                    for cg in range(EA // 512):
                        def evf(psv, w, cg=cg):
                            step("act", lambda e: e.activation(out=TU[:], in_=psv, func=AF.Gelu), waits=w)
                            step("dve", lambda e: e.tensor_tensor(out=SA[:, 4 * cg:4 * cg + 4, :],
                                                                  in0=SA[:, 4 * cg:4 * cg + 4, :], in1=TU[:], op=ALU.mult))
                        job(W1_d[cg], KC, lambda wt, banks: fm_mms(wt, banks, xt1, KC), evf, pe_waits=xw)
                    for cg in range(EA // 512):
                        def evf(psv, w, cg=cg):
                            step("act", lambda e: e.activation(out=TU[:], in_=psv, func=AF.Silu), waits=w)
                            return step("dve", lambda e: e.tensor_tensor(out=aT[:, 4 * cg:4 * cg + 4, :],
                                                                         in0=SA[:, 4 * cg:4 * cg + 4, :], in1=TU[:],
                                                                         op=ALU.mult),
                                        waits=[(SD[2], at_sd[0])] if cg == 0 else [])
                        hold = {}

                        def evf3(psv, w, evf=evf, hold=hold):
                            hold["s"] = evf(psv, w)
                        job(W1_d[OFF_ZA // 512 + cg], KC, lambda wt, banks: fm_mms(wt, banks, xt1, KC), evf3, pe_waits=xw)
                    at_sd[0] = sp_dma(AT_d[:, :, t0:t0 + T].rearrange("k p t -> p k t"), aT[:], 2, [(EV, hold["s"])])
                groups = [0, 1, 2] if 1 <= e_ <= 8 else [2]
                for g in groups:
                    for hq in range(HPG // 4):
                        kg = g * (HPG // 4) + hq

                        lo = 0
                        nt_ = T
                        if g == 0 and e_ == 1:
                            lo, nt_ = T - 128, 128
                        if g == 0 and e_ == 8:
                            lo, nt_ = 0, 128

                        def evf(psv, w, kg=kg, e_=e_, lo=lo, nt_=nt_):
                            out_stage("dve", lambda e, s: e.tensor_copy(out=s[:, :, 0:nt_], in_=psv[:, :, 0:nt_]),
                                      lambda: [(KT_d[4 * kg:4 * kg + 4, :, e_ * T + lo:e_ * T + lo + nt_]
                                                .rearrange("s p t -> p s t"), lambda s: s[:, :, 0:nt_])], w)
                        job(W1_d[OFF_K // 512 + kg], KC,
                            lambda wt, banks, lo=lo, nt_=nt_: fm_mms(wt, banks, xt1[:, :, lo:lo + nt_], KC, ntok=nt_),
                            evf, pe_waits=xw)
                for g in groups:
                    for hq in range(HPG // 4):
                        vg = g * (HPG // 4) + hq
                        if g == 0:
                            lf = lambda kc, tb: xt1[:, kc, 128 * tb:128 * tb + 128]
                        else:
                            lf = lambda kc, tb: xt1[:, kc, tb::4]

                        tbs = [0, 1, 2, 3]
                        if g == 0 and e_ == 1:
                            tbs = [3]
                        if g == 0 and e_ == 8:
                            tbs = [0]

                        def dsts(g=g, hq=hq, e_=e_, tbs=tbs):
                            cs = slice(hq * 512, (hq + 1) * 512)
                            if g == 0:
                                t_lo, t_n = tbs[0], len(tbs)
                                return [(Vs_d[0, e_ * T + 128 * t_lo:e_ * T + 128 * (t_lo + t_n), cs]
                                         .rearrange("(t p) c -> p t c", p=128), lambda s: s[:, t_lo:t_lo + t_n, :])]
                            if g == 1:
                                return [(Vs_d[1].rearrange("(r n) c -> n r c", r=4)[128 * e_:128 * e_ + 128, :, cs],
                                         lambda s: s)]
                            v2 = Vs_d[2].rearrange("(r n) c -> n r c", r=16)
                            return [(v2[32 * e_:32 * e_ + 32, 4 * i:4 * i + 4, cs], (lambda s, i=i: s[i::4]))
                                    for i in range(4)]

                        def evf(psv, w, dsts=dsts, tbs=tbs):
                            a, b_ = tbs[0], tbs[0] + len(tbs)
                            out_stage("act", lambda e, s: e.copy(out=s[:, a:b_, :], in_=psv[:, a:b_, :]), dsts, w)
                        job(W1_d[OFF_VB // 512 + vg], KC,
                            lambda wt, banks, lf=lf, tbs=tbs: tm_mms(wt, banks, lf, KC, tbs=tbs), evf, pe_waits=xw)
                flush_x()
            p1_sd1 = st["sd"][1] * 16
            p1_sd2 = st["sd"][2] * 16
            p1_sd3 = st["sd"][3] * 16
            p1_ev = st["ev"]
            p1_jn = st["jn"]
        marks.append({k: len(v) for k, v in P.q.items()})

        n1 = EXT // 4
        NS = NE = NP = NO = 3
        with ExitStack() as es4:
            Q = [es4.enter_context(nc.semaphore("Q%d" % i)) for i in range(6)]
            QF = es4.enter_context(nc.semaphore("QF"))
            LD = [es4.enter_context(nc.semaphore("LD%d" % i)) for i in range(2)]
            kts = [es4.enter_context(nc.sbuf_tensor("kt%d" % i, [128, EXT], BF16)) for i in range(2)]
            qts = [es4.enter_context(nc.sbuf_tensor("qt%d" % i, [128, OWN], BF16)) for i in range(2)]
            vbs = [es4.enter_context(nc.sbuf_tensor("vb%d" % i, [128, 50 * 128], BF16)) for i in range(2)]
            ebs = [es4.enter_context(nc.sbuf_tensor("eb%d" % i, [128, 1, 256], F32)) for i in range(2)]
            Efs = [es4.enter_context(nc.sbuf_tensor("Ef%d" % i, [128, 2, 256], F32)) for i in range(NE)]
            Pbs = [es4.enter_context(nc.sbuf_tensor("Pb%d" % i, [128, 2, 256], BF16)) for i in range(NP)]
            numaccs = [es4.enter_context(nc.sbuf_tensor("numacc%d" % i, [128, OWN], F32)) for i in range(2)]
            denaccs = [es4.enter_context(nc.sbuf_tensor("denacc%d" % i, [128, OWN], F32)) for i in range(2)]
            OSD = [es4.enter_context(nc.semaphore("OSD%d" % i)) for i in range(2)]
            sks = es4.enter_context(nc.sbuf_tensor("sks", [2, EXT], BF16))
            sqs = es4.enter_context(nc.sbuf_tensor("sqs", [2, OWN], BF16))
            ones = es4.enter_context(nc.sbuf_tensor("ones", [128, 128], BF16))

            p1w = [(EV, p1_ev), (MM, p1_jn), (SD[1], p1_sd1), (SD[2], p1_sd2), (SD[3], p1_sd3), (SD[0], eb_done0)]
            m1 = pool_load(sks[:], sk_d[:, :], waits=p1w)
            m2 = pool_load(sqs[:], sq_d[:, :])
            for (sem, val) in p1w:
                P.wait("sp", sem, val)
            for (sem, val) in p1w:
                P.wait("dve", sem, val)
            P.emit("dve", lambda e: e.memset(ones[:], 1.0).then_inc(QF, 1))
            qf = 1
            for eng in ("pe", "act"):
                for (sem, val) in p1w:
                    P.wait(eng, sem, val)

            units = []
            ldcount = [0, 0]
            hg_last_unit = {}
            hgi = 0
            for h in range(HPG):
                for g in range(3):
                    set_ = hgi % 2
                    kt, qt, vb, eb = kts[set_], qts[set_], vbs[set_], ebs[set_]
                    H = g * HPG + h
                    hc = slice(h * 128, (h + 1) * 128)
                    loads = [(kt[:], KT_d[H]), (qt[:], QT_d[H]),
                             (eb[:, 0, :], EB_d[g, h].rearrange("(k c) -> k c", c=256))]
                    ul = []
                    if g == 0:
                        vt0 = vb[:, 0:25 * 128].rearrange("p (i c) -> p i c", c=128)
                        loads.append((vt0, Vs_d[0, 960:960 + 25 * 128, hc].rearrange("(i p) c -> p i c", p=128)))
                        for u in range(12):
                            blocks = []
                            for i in range(2):
                                j = 2 * u + i
                                blocks.append((slice(128 * j, 128 * j + 128), 128,
                                               [(slice(960 + 128 * j, 1088 + 128 * j), 128, vt0[:, j, :]),
                                                (slice(1088 + 128 * j, 1216 + 128 * j), 128, vt0[:, j + 1, :])]))
                            ul.append((blocks, (lambda t, u=u: t[:, 256 * u:256 * u + 256]), 256))
                    elif g == 1:
                        vt1 = vb[:, 0:28 * 128].rearrange("p (r i c) -> p r i c", r=4, c=128)
                        for r in range(4):
                            loads.append((vt1[:, r], Vs_d[1, r * n1 + 192:r * n1 + 192 + 7 * 128, hc]
                                          .rearrange("(i p) c -> p i c", p=128)))
                        for r in range(4):
                            for pi in range(3):
                                blocks = []
                                for i in range(2):
                                    b = 2 * pi + i
                                    ka = r + 4 * (192 + 128 * b)
                                    kb = r + 4 * (320 + 128 * b)
                                    blocks.append((sl(r + 512 * b, 128, 4), 128,
                                                   [(sl(ka, 128, 4), 128, vt1[:, r, b, :]),
                                                    (sl(kb, 128, 4), 128, vt1[:, r, b + 1, :])]))
                                ul.append((blocks, (lambda t, r=r, pi=pi: t[:, sl(r + 1024 * pi, 256, 4)]), 256))
                    else:
                        vt2 = vb[:, 0:32 * 128].rearrange("p (r j c) -> p r j c", r=16, c=128)
                        vt2c = vb[0:64, 32 * 128:48 * 128].rearrange("p (r c) -> p r c", c=128)
                        v2 = Vs_d[2].rearrange("(r n) c -> n r c", r=16)
                        for j in range(2):
                            loads.append((vt2[:, :, j, :], v2[128 * j:128 * j + 128, :, hc]))
                        loads.append((vt2c, v2[256:320, :, hc]))
                        for r in range(16):
                            blocks = []
                            for qb, nq in ((0, 128), (1, 64)):
                                kl = []
                                for ab in range(2):
                                    j = qb + ab
                                    nk = 128 if j < 2 else 64
                                    vt = vt2[:, r, j, :] if j < 2 else vt2c[:, r, :]
                                    kl.append((sl(r + 2048 * j, nk, 16), nk, vt))
                                blocks.append((sl(r + 2048 * qb, nq, 16), nq, kl))
                            ul.append((blocks, (lambda t, r=r: t[:, sl(r, 192, 16)]), 192))
                    for ui, (blocks, accout, ncols) in enumerate(ul):
                        units.append(dict(hgi=hgi, set=set_, g=g, h=h, blocks=blocks, accout=accout, ncols=ncols,
                                          first=(ui == 0), loads=loads if ui == 0 else None,
                                          last_head=(g == 2 and ui == len(ul) - 1)))
                    hg_last_unit[hgi] = len(units) - 1
                    hgi += 1
            NU = len(units)
            ldval = {}

            def s1(n):
                u = units[n]
                kt, qt = kts[u["set"]], qts[u["set"]]
                if u["first"]:
                    i = u["hgi"]
                    if i >= 2:
                        P.wait("sp", Q[4], hg_last_unit[i - 2] + 1)
                    for (dst, src) in u["loads"]:
                        P.emit("sp", lambda e, dst=dst, src=src, s=u["set"]: e.dma_start(out=dst, in_=src).then_inc(LD[s], 16))
                        ldcount[u["set"]] += 1
                    P.wait("pe", LD[u["set"]], 16 * ldcount[u["set"]])
                    if n == 0:
                        P.wait("pe", PL, m2)
                P.wait("pe", Q[2], n - NS + 1)
                Sb = PS[:, n % NS, :]
                fns = []
                for bi, (qs, nq, kl) in enumerate(u["blocks"]):
                    for ab, (ks, nk, vt) in enumerate(kl):
                        c0 = bi * 256 + ab * 128
                        fns.append(mm(Sb[0:nk, c0:c0 + nq], kt[:, ks], qt[:, qs], True, False, skip=True))
                        fns.append(mm(Sb[0:nk, c0:c0 + nq], sks[:, ks], sqs[:, qs], False, True, skip=True))
                for f in fns[:-1]:
                    P.emit("pe", f)
                P.emit("pe", lambda e, f=fns[-1]: f(e).then_inc(Q[1], 1))

            def s2(n):
                P.wait("act", Q[1], n + 1)
                P.wait("act", Q[3], n - NE + 1)
                Sb = PS[:, n % NS, :]
                Ef = Efs[n % NE]
                P.emit("act", lambda e: e.activation(out=Ef[:].rearrange("p b c -> p (b c)"), in_=Sb,
                                                     func=AF.Exp).then_inc(Q[2], 1))

            def s3(n):
                u = units[n]
                P.wait("dve", Q[2], n + 1)
                P.wait("dve", Q[4], n - NP + 1)
                Ef, Pb, eb = Efs[n % NE], Pbs[n % NP], ebs[u["set"]]
                P.emit("dve", lambda e: e.tensor_tensor(out=Pb[:], in0=Ef[:], in1=eb[:].to_broadcast([128, 2, 256]),
                                                        op=ALU.mult).then_inc(Q[3], 1))

            def s4(n):
                u = units[n]
                P.wait("pe", Q[3], n + 1)
                P.wait("pe", Q[5], n - NO + 1)
                OD = PS[:, NS + n % NO, :]
                Pb = Pbs[n % NP]
                fns = []
                k0 = True
                for bi, (qs, nq, kl) in enumerate(u["blocks"]):
                    for ab, (ks, nk, vt) in enumerate(kl):
                        rhs = Pb[0:nk, bi, ab * 128:ab * 128 + nq]
                        fns.append(mm(OD[:, bi * 128:bi * 128 + nq], vt[0:nk, :], rhs, k0, False, skip=True))
                        k0 = False
                        fns.append(mm(OD[:, 256 + bi * 128:256 + bi * 128 + nq], ones[0:nk, :], rhs, False, False, skip=True))
                for f in fns[:-1]:
                    P.emit("pe", f)
                P.emit("pe", lambda e, f=fns[-1]: f(e).then_inc(Q[4], 1))

            osd = [0, 0]

            def s5(n):
                nonlocal qf
                u = units[n]
                P.wait("dve", Q[4], n + 1)
                P.wait("dve", Q[5], n)
                OD = PS[:, NS + n % NO, :]
                nco = u["ncols"]
                ao = u["accout"]
                hb = u["h"] % 2
                numacc, denacc = numaccs[hb], denaccs[hb]
                if u["g"] == 0:
                    if u["first"]:
                        P.wait("dve", OSD[hb], osd[hb])
                    P.emit("dve", lambda e: e.tensor_copy(out=ao(numacc[:]), in_=OD[:, 0:nco]))
                    P.emit("dve", lambda e: e.tensor_copy(out=ao(denacc[:]), in_=OD[:, 256:256 + nco]).then_inc(Q[5], 1))
                else:
                    P.emit("dve", lambda e: e.tensor_tensor(out=ao(numacc[:]), in0=ao(numacc[:]), in1=OD[:, 0:nco],
                                                            op=ALU.add))
                    P.emit("dve", lambda e: e.tensor_tensor(out=ao(denacc[:]), in0=ao(denacc[:]),
                                                            in1=OD[:, 256:256 + nco], op=ALU.add).then_inc(Q[5], 1))
                if u["last_head"]:
                    h = u["h"]
                    P.wait("dve", Q[5], n + 1)
                    P.emit("dve", lambda e: e.reciprocal(out=denacc[:], in_=denacc[:]).then_inc(QF, 1))
                    qf += 1
                    P.wait("dve", QF, qf)
                    P.emit("dve", lambda e: e.tensor_tensor(out=numacc[:], in0=numacc[:], in1=denacc[:],
                                                            op=ALU.mult).then_inc(QF, 1))
                    qf += 1
                    P.wait("pool", QF, qf)
                    P.emit("pool", lambda e: e.dma_start(out=OT_d[h], in_=numacc[:]).then_inc(OSD[hb], 16))
                    osd[hb] += 16

            for t in range(NU + 4):
                if t < NU:
                    s1(t)
                if 0 <= t - 1 < NU:
                    s2(t - 1)
                if 0 <= t - 2 < NU:
                    s3(t - 2)
                if 0 <= t - 3 < NU:
                    s4(t - 3)
                if 0 <= t - 4 < NU:
                    s5(t - 4)
            P.wait("dve", QF, qf)
            P.emit("dve", lambda e: e.memset(ones[:], 1.0).then_inc(CH, 1))
            st["ch"] += 1
            P.wait("sp", Q[4], NU)
            p2_ch = st["ch"]
            p2_osd = list(osd)
        marks.append({k: len(v) for k, v in P.q.items()})

        AR = max(KC * 512 + KA * 512 + HPG * 512, 8 * D)
        RS = max(2 * D, 6144)
        with ExitStack() as es5:
            arena = es5.enter_context(nc.sbuf_tensor("arena", [128, AR], BF16))
            mT = es5.enter_context(nc.sbuf_tensor("mT", [128, KC, 512], BF16))
            R = es5.enter_context(nc.sbuf_tensor("R", [128, RS], F32))
            stats3 = es5.enter_context(nc.sbuf_tensor("stats3", [128, 8, 6], F32))
            mv3 = es5.enter_context(nc.sbuf_tensor("mv3", [128, 2], F32))
            rstd3 = es5.enter_context(nc.sbuf_tensor("rstd3", [128, 1], F32))
            xt3 = arena[:, 0:KC * 512].rearrange("p (k t) -> p k t", t=512)
            aT3 = arena[:, KC * 512:KC * 512 + KA * 512].rearrange("p (k t) -> p k t", t=512)
            obT = arena[:, KC * 512 + KA * 512:KC * 512 + KA * 512 + HPG * 512].rearrange("p (k t) -> p k t", t=512)
            z = arena[:, 0:8 * D].bitcast(F32).rearrange("p (t c) -> p t c", t=4)
            SG = R[:, 0:2048].rearrange("p (s t) -> p s t", t=512)
            TA = R[:, 2048:4096].rearrange("p (s t) -> p s t", t=512)
            SGB = R[:, 4096:6144].rearrange("p (s t) -> p s t", t=512)
            gainb = R[:, 0:D]
            biasb = R[:, D:2 * D]
            for eng in ("sp", "pool"):
                P.wait(eng, CH, p2_ch)
                for i in range(2):
                    P.wait(eng, OSD[i], p2_osd[i])
            P.wait("pe", CH, p2_ch)
            P.wait("act", CH, p2_ch)
            P.wait("dve", CH, p2_ch)
            y_sd = 0
            y_sd1 = 0
            y_ev = st["ev"]
            for ti in range(OWN // T):
                t0 = ti * T
                xv, xdo = xload(xt3, xT_d[:, :, HALO + t0:HALO + t0 + T], st["jn"])
                xdo([(SD[2], y_sd), (EV, y_ev)])
                la = sp_load(aT3, AT_d[:, :, t0:t0 + T].rearrange("k p t -> p k t"),
                             waits=[(SD[2], y_sd), (SD[1], y_sd1), (EV, st["ev"]), (MM, st["jn"])])
                la = sp_load(obT, OT_d[:, :, t0:t0 + T].rearrange("k p t -> p k t"))
                xw = [(XL, xv)]
                for cg in range(EBW // 512):
                    def evf(psv, w, cg=cg):
                        step("act", lambda e: e.activation(out=SG, in_=psv, func=AF.Silu), waits=w)
                        return step("dve", lambda e: e.tensor_tensor(out=obT[:, 4 * cg:4 * cg + 4, :],
                                                                     in0=obT[:, 4 * cg:4 * cg + 4, :], in1=SG, op=ALU.mult),
                                    waits=[(SL, la)])
                    hold = {}

                    def evf3(psv, w, evf=evf, hold=hold):
                        hold["s"] = evf(psv, w)
                    job(W1_d[OFF_ZB // 512 + cg], KC, lambda wt, banks: fm_mms(wt, banks, xt3, KC), evf3, pe_waits=xw)
                s_ob = hold["s"]
                for dg in range(D // 512):
                    def ev_ga(psv, w):
                        step("act", lambda e: e.activation(out=SG, in_=psv, func=AF.Sigmoid), waits=w)
                    job(W1_d[OFF_GA // 512 + dg], KC, lambda wt, banks: fm_mms(wt, banks, xt3, KC), ev_ga, pe_waits=xw)

                    def ev_gb(psv, w):
                        step("act", lambda e: e.activation(out=SGB, in_=psv, func=AF.Sigmoid), waits=w)
                    job(W1_d[OFF_GB // 512 + dg], KC, lambda wt, banks: fm_mms(wt, banks, xt3, KC), ev_gb, pe_waits=xw)

                    def ev_pa(psv, w):
                        step("dve", lambda e: e.tensor_tensor(out=TA, in0=SG, in1=psv, op=ALU.mult), waits=w)
                    job(Wpa_d[dg], KA, lambda wt, banks: fm_mms(wt, banks, aT3, KA), ev_pa, pe_waits=[(SL, la)])

                    def ev_pb(psv, w, dg=dg):
                        step("dve", lambda e: e.tensor_tensor(out=SGB, in0=SGB, in1=psv, op=ALU.mult), waits=w)
                        step("dve", lambda e: e.tensor_tensor(out=mT[:, 4 * dg:4 * dg + 4, :], in0=TA, in1=SGB, op=ALU.add))
                    job(Wpb_d[dg], HPG, lambda wt, banks: fm_mms(wt, banks, obT, HPG), ev_pb, pe_waits=[(EV, s_ob)])
                s_m = st["ev"]
                jn_c = st["jn"]
                lz = sp_load(z, xown_d[t0:t0 + T, :].rearrange("(t p) c -> p t c", p=128),
                             waits=[(EV, s_m), (MM, jn_c)])
                lz = sp_load(gainb, lng_d.partition_broadcast(128)[:, 0, :])
                lz = sp_load(biasb, lnb_d.partition_broadcast(128)[:, 0, :])
                for og in range(D // 512):
                    def evf(psv, w, og=og):
                        step("dve", lambda e: e.scalar_tensor_tensor(out=z[:, :, og * 512:(og + 1) * 512],
                                                                     in0=z[:, :, og * 512:(og + 1) * 512], scalar=alpha,
                                                                     in1=psv, op0=ALU.mult, op1=ALU.add),
                             waits=w + [(SL, lz)])
                    job(Wout_d[og], KC,
                        lambda wt, banks: tm_mms(wt, banks, lambda kc, tb: mT[:, kc, 128 * tb:128 * tb + 128], KC),
                        evf, pe_waits=[(EV, s_m)])
                for tb in range(4):
                    nch = D // 512
                    for chn in range(nch):
                        step("dve", lambda e, tb=tb, chn=chn: e.bn_stats(out=stats3[:, chn, :],
                                                                       in_=z[:, tb, chn * 512:(chn + 1) * 512]))
                    step("dve", lambda e: e.bn_aggr(out=mv3[:], in_=stats3[:, 0:nch, :]))
                    step("dve", lambda e: e.tensor_scalar_add(out=rstd3[:], in0=mv3[:, 1:2], scalar1=LN_EPS))
                    step("act", lambda e: e.sqrt(out=rstd3[:], in_=rstd3[:]))
                    step("dve", lambda e: e.reciprocal(out=rstd3[:], in_=rstd3[:]))
                    step("dve", lambda e, tb=tb: e.scalar_tensor_tensor(out=z[:, tb, :], in0=z[:, tb, :],
                                                                       scalar=mv3[:, 0:1], in1=gainb,
                                                                       op0=ALU.subtract, op1=ALU.mult))
                    s_y = step("dve", lambda e, tb=tb: e.scalar_tensor_tensor(out=z[:, tb, :], in0=z[:, tb, :],
                                                                             scalar=rstd3[:, 0:1], in1=biasb,
                                                                             op0=ALU.mult, op1=ALU.add))
                    if tb < 2:
                        y_sd = sp_dma(y_d[t0 + 128 * tb:t0 + 128 * tb + 128, :], z[:, tb, :], 2, [(EV, s_y)])
                        y_ev = s_y
                    else:
                        y_sd1 = sp_dma(y_d[t0 + 128 * tb:t0 + 128 * tb + 128, :], z[:, tb, :], 1, [(EV, s_y)])
            P.wait("sp", SD[2], y_sd)
            P.wait("sp", SD[1], y_sd1)

        marks.append({k: len(v) for k, v in P.q.items()})
        ph = cfg.get("phases", 4)
        lim = marks[ph - 1] if ph > 0 else {k: 0 for k in P.q}
        for k in P.q:
            P.q[k] = P.q[k][:lim[k]]
        with nc.Block() as block:
            @block.tensor
            def _(e):
                for f in P.q["pe"]:
                    f(e)

            @block.scalar
            def _(e):
                for f in P.q["act"]:
                    f(e)

            @block.vector
            def _(e):
                for f in P.q["dve"]:
                    f(e)

            @block.sync
            def _(e):
                for f in P.q["sp"]:
                    f(e)

            @block.gpsimd
            def _(e):
                for f in P.q["pool"]:
                    f(e)
    return nc


def prep_inputs(cfg, x_prompt, x_sample, w_in, w_spatial, b_spatial, ln_v_gain, ln_v_bias, rel_bias,
                w_proj_a, w_proj_b, w_out, ln_gain, ln_bias):
    D, EA, HPG = cfg["D"], cfg["EA"], cfg["HPG"]
    KC, KA, GA = D // 128, EA // 128, EA // 256
    f = np.float32
    xf = np.concatenate([np.asarray(x_prompt, f).reshape(-1, D), np.asarray(x_sample, f).reshape(-1, D)], axis=0)
    xpad = np.concatenate([np.zeros((HALO, D), f), xf, np.zeros((HALO, D), f)], axis=0)

    def gran(w, kcn):
        K, N = w.shape
        return np.ascontiguousarray(w.reshape(kcn, 128, N // 512, 512).transpose(2, 1, 0, 3))

    shared = {
        "W1": gran(np.asarray(w_in[0], f), KC),
        "Wpa": gran(np.asarray(w_proj_a[0], f), KA),
        "Wpb": gran(np.asarray(w_proj_b[0], f), HPG),
        "Wout": gran(np.asarray(w_out[0], f), KC),
        "WsT": np.ascontiguousarray(np.asarray(w_spatial[0], f).transpose(2, 0, 1).reshape(128, GA * 128)),
        "bsp": np.asarray(b_spatial[0], f).reshape(1, GA * 128),
        "lnvg": np.asarray(ln_v_gain[0], f).reshape(1, EA),
        "lnvb": np.asarray(ln_v_bias[0], f).reshape(1, EA),
        "lng": np.asarray(ln_gain[0], f).reshape(1, D),
        "lnb": np.asarray(ln_bias[0], f).reshape(1, D),
        "relb": np.ascontiguousarray(np.asarray(rel_bias, f)),
        "OH": make_onehot(),
    }
    in_maps = []
    for c in range(NCORE):
        ext = xpad[c * OWN:c * OWN + EXT]
        xT = np.ascontiguousarray(ext.T.reshape(KC, 128, EXT).transpose(1, 0, 2))
        flat = np.arange(EXT) + c * OWN - HALO
        seq = np.floor_divide(flat, SEQ)
        seq = np.where(flat < 0, -1, np.where(flat >= TOK, NSEQ, seq))
        s = (seq == seq[-1]).astype(f)
        sk = np.stack([s, 1.0 - s], axis=0)
        so = s[HALO:HALO + OWN]
        sq = np.stack([-BIG * (1.0 - so), -BIG * so], axis=0).astype(f)
        m = dict(shared)
        m.update({"xT": xT, "xown": np.ascontiguousarray(xf[c * OWN:(c + 1) * OWN]), "sk": sk.astype(f), "sq": sq})
        in_maps.append(m)
    return in_maps


def run(cfg, **inputs):
    nc = build(cfg)
    in_maps = prep_inputs(cfg, **inputs)
    res = run_bass_kernel_spmd(nc, in_maps, core_ids=list(range(NCORE)))
    y = np.concatenate([np.asarray(r["y"]) for r in res.results], axis=0)
    D = cfg["D"]
    nb = np.asarray(inputs["x_prompt"]).shape[0]
    yp = y[:nb * SEQ].reshape(nb, SEQ, D).astype(np.float32)
    ys = y[nb * SEQ:].reshape(-1, SEQ, D).astype(np.float32)
    return (yp, ys)


def kernel(**inputs):
    return run(CFG, **inputs)
```
